# Optimizing a Trainium2 kernel written in Bass

```python
import jax, jax.numpy as jnp
from jax import lax
import numpy as np

D_MODEL = 2048
BATCH = 4
SEQ = 4096
DEPTH = 2

HEAD_DIM = 128
A_HEADS = 6
B_HEADS = 5
C_HEADS = 5
A_WIDTH = A_HEADS * HEAD_DIM
B_WIDTH = B_HEADS * HEAD_DIM
C_WIDTH = C_HEADS * HEAD_DIM
MIX_WIDTH = A_WIDTH + B_WIDTH + C_WIDTH
N_BRANCHES = 3
MOBA_BLOCK = 256
MOBA_TOPK = 3
MOBA_Q_CHUNK = 32
ROPE_THETA = 500000.0
ROPE_DIM = HEAD_DIM // 4
RET_THETA = 10000.0
RET_CHUNK = 128
GDN_CHUNK = 64
CONV_WIDTH = 4
FFN_HIDDEN = -(-8 * D_MODEL // (3 * 256)) * 256
NORM_EPS = 1e-6
SPLIT_SIZES = (A_WIDTH,) * 3 + (B_WIDTH,) * 4 + (C_WIDTH,) * 4 + (C_HEADS, C_HEADS, N_BRANCHES * D_MODEL)
IN_COLS = sum(SPLIT_SIZES)

kernel_name = 'hybrid_moba_retention_gdn_gated_block'


def rmsnorm(x, g):
    xf = x.astype(jnp.float32)
    xf = xf * lax.rsqrt(jnp.mean(xf * xf, axis=-1, keepdims=True) + NORM_EPS)
    return xf.astype(x.dtype) * g


def l2norm(x):
    return x * lax.rsqrt(jnp.sum(x * x, axis=-1, keepdims=True) + NORM_EPS)


def split_heads(t, n):
    b, s, _ = t.shape
    return t.reshape(b, s, n, HEAD_DIM).transpose(0, 2, 1, 3)


def merge_heads(t):
    b, h, s, d = t.shape
    return t.transpose(0, 2, 1, 3).reshape(b, s, h * d)


def apply_rotary(x, inv_freq):
    rot = 2 * inv_freq.shape[0]
    pos = jnp.arange(x.shape[2], dtype=jnp.float32)
    ang = pos[:, None] * inv_freq[None, :]
    ang = jnp.concatenate([ang, ang], axis=-1)
    cos, sin = jnp.cos(ang), jnp.sin(ang)
    xr = x[..., :rot].astype(jnp.float32)
    x1, x2 = xr[..., : rot // 2], xr[..., rot // 2:]
    rotated = xr * cos + jnp.concatenate([-x2, x1], axis=-1) * sin
    return jnp.concatenate([rotated.astype(x.dtype), x[..., rot:]], axis=-1)


def moba_attention(q, k, v):
    b, h, t, d = q.shape
    nb = -(-t // MOBA_BLOCK)
    tp = nb * MOBA_BLOCK
    if tp != t:
        pad = ((0, 0), (0, 0), (0, tp - t), (0, 0))
        q, k, v = jnp.pad(q, pad), jnp.pad(k, pad), jnp.pad(v, pad)
    scale = d ** -0.5
    k_blocks = k.reshape(b, h, nb, MOBA_BLOCK, d)
    v_blocks = v.reshape(b, h, nb, MOBA_BLOCK, d)
    k_mean = jnp.mean(k_blocks.astype(jnp.float32), axis=3)
    gate = jnp.einsum('bhtd,bhnd->bhtn', q.astype(jnp.float32), k_mean)
    q_block = jnp.arange(tp) // MOBA_BLOCK
    fully_past = jnp.arange(nb)[None, :] < q_block[:, None]
    gate = jnp.where(fully_past, gate, -jnp.inf)
    n_sel = min(MOBA_TOPK, nb)
    _, top_idx = lax.top_k(gate, n_sel)
    bi = jnp.arange(b)[:, None, None, None]
    hi = jnp.arange(h)[None, :, None, None]
    n_chunks = tp // MOBA_Q_CHUNK

    def one_chunk(c):
        start = c * MOBA_Q_CHUNK
        own = start // MOBA_BLOCK
        q_c = lax.dynamic_slice_in_dim(q, start, MOBA_Q_CHUNK, axis=2).astype(jnp.float32) * scale
        idx_c = lax.dynamic_slice_in_dim(top_idx, start, MOBA_Q_CHUNK, axis=2)
        k_sel = k_blocks[bi, hi, idx_c]
        v_sel = v_blocks[bi, hi, idx_c]
        k_own = lax.dynamic_slice_in_dim(k, own * MOBA_BLOCK, MOBA_BLOCK, axis=2)
        v_own = lax.dynamic_slice_in_dim(v, own * MOBA_BLOCK, MOBA_BLOCK, axis=2)
        s_sel = jnp.einsum('bhqd,bhqnsd->bhqns', q_c, k_sel)
        s_sel = jnp.where((idx_c < own)[..., None], s_sel, -jnp.inf)
        s_own = jnp.einsum('bhqd,bhsd->bhqs', q_c, k_own)
        q_pos = start + jnp.arange(MOBA_Q_CHUNK)
        k_pos = own * MOBA_BLOCK + jnp.arange(MOBA_BLOCK)
        s_own = jnp.where(k_pos[None, :] <= q_pos[:, None], s_own, -jnp.inf)
        scores = jnp.concatenate([s_sel.reshape(b, h, MOBA_Q_CHUNK, n_sel * MOBA_BLOCK), s_own], axis=-1)
        p = jax.nn.softmax(scores, axis=-1)
        p_sel = p[..., : n_sel * MOBA_BLOCK].reshape(b, h, MOBA_Q_CHUNK, n_sel, MOBA_BLOCK)
        p_own = p[..., n_sel * MOBA_BLOCK:]
        return (jnp.einsum('bhqns,bhqnsd->bhqd', p_sel, v_sel)
                + jnp.einsum('bhqs,bhsd->bhqd', p_own, v_own))

    o = lax.map(one_chunk, jnp.arange(n_chunks))
    o = o.transpose(1, 2, 0, 3, 4).reshape(b, h, tp, d)[:, :, :t]
    return o.astype(v.dtype)


def retention_chunked(q, k, v, log_gamma):
    b, h, t, dk = q.shape
    dv = v.shape[-1]
    c = RET_CHUNK
    n = t // c
    k = k * dk ** -0.5
    q, k, v = (a.reshape(b, h, n, c, a.shape[-1]) for a in (q, k, v))
    i = jnp.arange(c, dtype=jnp.float32)
    lg = log_gamma[:, None]
    tril = jnp.tril(jnp.ones((c, c), dtype=bool))
    rel = i[:, None] - i[None, :]
    decay = jnp.where(tril, jnp.exp(jnp.where(tril, lg[:, :, None] * rel, 0.0)), 0.0)
    scores = jnp.einsum('bhncd,bhnsd->bhncs', q, k) * decay[None, :, None]
    inner = jnp.einsum('bhncs,bhnse->bhnce', scores, v)
    zeta = jnp.exp(lg * (c - 1 - i))
    chunk_state = jnp.einsum('bhnsd,bhnse->bhnde', k * zeta[None, :, None, :, None], v)
    gamma_c = jnp.exp(log_gamma * c)[None, :, None, None]

    def step(state, u):
        return state * gamma_c + u, state

    _, prev = lax.scan(step, jnp.zeros((b, h, dk, dv), jnp.float32), jnp.moveaxis(chunk_state, 2, 0))
    prev = jnp.moveaxis(prev, 0, 2)
    xi = jnp.exp(lg * (i + 1.0))
    cross = jnp.einsum('bhncd,bhnde->bhnce', q, prev) * xi[None, :, None, :, None]
    return (inner + cross).reshape(b, h, t, dv)


def causal_depthwise_conv(x, w):
    kw, ch = w.shape
    return lax.conv_general_dilated(x, w[:, None, :], window_strides=(1,), padding=[(kw - 1, 0)],
                                    dimension_numbers=('NWC', 'WIO', 'NWC'), feature_group_count=ch)


def gated_delta_rule_chunked(q, k, v, beta, g):
    b, h, t, dk = q.shape
    dv = v.shape[-1]
    c = GDN_CHUNK
    n = t // c
    q = q * dk ** -0.5
    q, k, v = (a.reshape(b, h, n, c, a.shape[-1]) for a in (q, k, v))
    beta = beta.reshape(b, h, n, c)
    gcum = jnp.cumsum(g.reshape(b, h, n, c), axis=-1)
    tril = jnp.tril(jnp.ones((c, c), dtype=bool))
    strict = jnp.tril(jnp.ones((c, c), dtype=bool), -1)
    diff = gcum[..., :, None] - gcum[..., None, :]
    lmask = jnp.where(tril, jnp.exp(jnp.where(tril, diff, 0.0)), 0.0)
    k_beta = k * beta[..., None]
    v_beta = v * beta[..., None]
    a_mat = jnp.where(strict, jnp.einsum('bhncd,bhnsd->bhncs', k_beta, k) * lmask, 0.0)
    rhs = jnp.concatenate([v_beta, k_beta * jnp.exp(gcum)[..., None]], axis=-1)
    sol = lax.linalg.triangular_solve(a_mat + jnp.eye(c, dtype=a_mat.dtype), rhs,
                                      left_side=True, lower=True, unit_diagonal=True)
    u, w = sol[..., :dv], sol[..., dv:]
    attn_intra = jnp.einsum('bhncd,bhnsd->bhncs', q, k) * lmask
    q_dec = q * jnp.exp(gcum)[..., None]
    g_last = gcum[..., -1]
    k_dec = k * jnp.exp(g_last[..., None] - gcum)[..., None]

    def step(state, xs):
        q_n, w_n, u_n, attn_n, k_n, gl_n = xs
        v_new = u_n - jnp.einsum('bhcd,bhde->bhce', w_n, state)
        o_n = jnp.einsum('bhcd,bhde->bhce', q_n, state) + jnp.einsum('bhcs,bhse->bhce', attn_n, v_new)
        state = state * jnp.exp(gl_n)[..., None, None] + jnp.einsum('bhcd,bhce->bhde', k_n, v_new)
        return state, o_n

    xs = tuple(jnp.moveaxis(a, 2, 0) for a in (q_dec, w, u, attn_intra, k_dec, g_last))
    _, o = lax.scan(step, jnp.zeros((b, h, dk, dv), jnp.float32), xs)
    return jnp.moveaxis(o, 0, 2).reshape(b, h, t, dv)


def setup_inputs(seed: int = 0) -> dict:
    key = jax.random.key(seed)
    ks = jax.random.split(key, 20)
    f32 = jnp.float32
    nrm = lambda k_, shape, s: jax.random.normal(k_, shape, f32) * s
    dt = jnp.exp(jax.random.uniform(ks[9], (DEPTH, C_HEADS), f32, minval=np.log(1e-3), maxval=np.log(1e-1)))
    return {
        'x': jax.random.normal(ks[0], (BATCH, SEQ, D_MODEL), f32),
        'attn_norm': 1.0 + nrm(ks[1], (DEPTH, D_MODEL), 0.02),
        'w_in': nrm(ks[2], (DEPTH, D_MODEL, IN_COLS), D_MODEL ** -0.5),
        'q_norm': 1.0 + nrm(ks[3], (DEPTH, HEAD_DIM), 0.02),
        'k_norm': 1.0 + nrm(ks[4], (DEPTH, HEAD_DIM), 0.02),
        'ret_norm': 1.0 + nrm(ks[5], (DEPTH, B_WIDTH), 0.02),
        'conv_w': nrm(ks[6], (DEPTH, CONV_WIDTH, 3 * C_WIDTH), CONV_WIDTH ** -0.5),
        'a_log': jnp.log(jax.random.uniform(ks[7], (DEPTH, C_HEADS), f32, minval=1.0, maxval=16.0)),
        'dt_bias': dt + jnp.log(-jnp.expm1(-dt)),
        'gdn_norm': 1.0 + nrm(ks[8], (DEPTH, HEAD_DIM), 0.02),
        'w_branch': nrm(ks[10], (DEPTH, MIX_WIDTH, D_MODEL), C_WIDTH ** -0.5),
        'w_out': nrm(ks[11], (DEPTH, D_MODEL, D_MODEL), D_MODEL ** -0.5),
        'ffn_norm': 1.0 + nrm(ks[12], (DEPTH, D_MODEL), 0.02),
        'w_gate': nrm(ks[13], (DEPTH, D_MODEL, FFN_HIDDEN), D_MODEL ** -0.5),
        'w_up': nrm(ks[14], (DEPTH, D_MODEL, FFN_HIDDEN), D_MODEL ** -0.5),
        'w_down': nrm(ks[15], (DEPTH, FFN_HIDDEN, D_MODEL), FFN_HIDDEN ** -0.5),
    }


def reference(x, attn_norm, w_in, q_norm, k_norm, ret_norm, conv_w, a_log, dt_bias, gdn_norm,
              w_branch, w_out, ffn_norm, w_gate, w_up, w_down):
    f32 = jnp.float32
    b, t, _ = x.shape
    split_points = [int(s) for s in np.cumsum(SPLIT_SIZES)[:-1]]
    rope_freq = ROPE_THETA ** (-jnp.arange(0, ROPE_DIM, 2, dtype=f32) / ROPE_DIM)
    ret_freq = RET_THETA ** (-jnp.linspace(0.0, 1.0, HEAD_DIM // 2, dtype=f32))
    ret_log_gamma = jnp.log1p(-jnp.exp2(-5.0 - jnp.arange(B_HEADS, dtype=f32)))

    for layer in range(DEPTH):
        h = rmsnorm(x, attn_norm[layer])
        proj = h @ w_in[layer]
        (qa, ka, va, qb, kb, vb, gb, qc, kc, vc, zc, beta_logit, alpha_logit,
         gate_logit) = jnp.split(proj, split_points, axis=-1)

        qa_h = apply_rotary(rmsnorm(split_heads(qa, A_HEADS), q_norm[layer]), rope_freq)
        ka_h = apply_rotary(rmsnorm(split_heads(ka, A_HEADS), k_norm[layer]), rope_freq)
        o_a = merge_heads(moba_attention(qa_h, ka_h, split_heads(va, A_HEADS)))

        qb_h = apply_rotary(split_heads(qb, B_HEADS), ret_freq).astype(f32)
        kb_h = apply_rotary(split_heads(kb, B_HEADS), ret_freq).astype(f32)
        ret = retention_chunked(qb_h, kb_h, split_heads(vb, B_HEADS).astype(f32), ret_log_gamma)
        ret = rmsnorm(ret, ret_norm[layer].reshape(B_HEADS, 1, HEAD_DIM).astype(f32))
        o_b = merge_heads(ret).astype(x.dtype) * jax.nn.silu(gb)

        qkv_c = jax.nn.silu(causal_depthwise_conv(jnp.concatenate([qc, kc, vc], axis=-1), conv_w[layer]))
        qc2, kc2, vc2 = jnp.split(qkv_c, 3, axis=-1)
        qc_h = l2norm(split_heads(qc2, C_HEADS).astype(f32))
        kc_h = l2norm(split_heads(kc2, C_HEADS).astype(f32))
        vc_h = split_heads(vc2, C_HEADS).astype(f32)
        beta = jax.nn.sigmoid(beta_logit.astype(f32)).transpose(0, 2, 1)
        g = (-jnp.exp(a_log[layer].astype(f32))
             * jax.nn.softplus(alpha_logit.astype(f32) + dt_bias[layer].astype(f32))).transpose(0, 2, 1)
        o_gdn = gated_delta_rule_chunked(qc_h, kc_h, vc_h, beta, g)
        o_gdn = rmsnorm(o_gdn, gdn_norm[layer].astype(f32))
        o_c = merge_heads(o_gdn).astype(x.dtype) * jax.nn.silu(zc)

        gates = jax.nn.sigmoid(gate_logit).reshape(b, t, N_BRANCHES, D_MODEL)
        wb = w_branch[layer]
        merged = (gates[:, :, 0] * (o_a @ wb[:A_WIDTH])
                  + gates[:, :, 1] * (o_b @ wb[A_WIDTH:A_WIDTH + B_WIDTH])
                  + gates[:, :, 2] * (o_c @ wb[A_WIDTH + B_WIDTH:]))
        x = x + merged @ w_out[layer]

        h2 = rmsnorm(x, ffn_norm[layer])
        x = x + (jax.nn.silu(h2 @ w_gate[layer]) * (h2 @ w_up[layer])) @ w_down[layer]
    return x
```

```python
import numpy as np
from contextlib import ExitStack
import concourse.bass as bass
import concourse.mybir as mybir
from concourse.bass_utils import run_bass_kernel_spmd

F32 = mybir.dt.float32
BF16 = mybir.dt.bfloat16
ALU = mybir.AluOpType
AF = mybir.ActivationFunctionType
AX = mybir.AxisListType


class Buf:
    __slots__ = ("t", "w", "r", "name")

    def __init__(self, t, name=""):
        self.t = t
        self.w = None
        self.r = {}
        self.name = name

    def __getitem__(self, k):
        return self.t[k]


class Eng:
    def __init__(self, name, is_dma=False, is_pe=False):
        self.name = name
        self.is_dma = is_dma
        self.is_pe = is_pe
        self.count = 0
        self.waited = {}
        self.prog = []
        self.sem = None
        self.sems = []
        self.vals = []
        self.n = 0


class Ctx:
    NRING = 8

    def __init__(self, nc):
        self.nc = nc
        self.es = ExitStack()
        self.semh = {}
        self.eng = {}
        for n in ("pe", "act", "dve", "pool"):
            e = Eng(n, is_pe=(n == "pe"))
            e.sem = self._sem("s_" + n)
            self.eng[n] = e
        for n in ("sp", "poolq"):
            e = Eng(n, is_dma=True)
            e.sems = [self._sem(f"d_{n}{i}") for i in range(self.NRING)]
            e.vals = [0] * self.NRING
            self.eng[n] = e
        self.stream = {"pe": "tensor", "act": "scalar", "dve": "vector", "pool": "gpsimd",
                       "sp": "sync", "poolq": "gpsimd"}
        self.order = {"tensor": [], "scalar": [], "vector": [], "gpsimd": [], "sync": []}
        self.nbuf = 0
        self.freed = []
        self.live = [[]]

    def _sem(self, name):
        h = self.es.enter_context(self.nc.semaphore(name))
        k = len(self.semh)
        self.semh[k] = h
        return k

    def sbuf(self, shape, dtype, name=None):
        self.nbuf += 1
        name = f"{name or 'sb'}_{self.nbuf}"
        t = self.es.enter_context(self.nc.sbuf_tensor(name, list(shape), dtype))
        return self._track(Buf(t, name))

    def psum(self, shape, dtype, name=None):
        self.nbuf += 1
        name = f"{name or 'ps'}_{self.nbuf}"
        t = self.es.enter_context(self.nc.psum_tensor(name, list(shape), dtype))
        return self._track(Buf(t, name))

    def _track(self, b):
        ml = self.nc.lookup_mloc(b.t)
        space = str(ml.type)
        lo = int(ml.addr) + (int(ml.bank) * 2048 if "PSUM" in space else 0)
        hi = lo + int(list(ml.dims)[1])
        for (sp, flo, fhi, ev) in self.freed:
            if sp == space and flo < hi and lo < fhi:
                for k, v in ev.items():
                    if b.r.get(k, 0) < v:
                        b.r[k] = v
        self.live[-1].append((b, (space, lo, hi)))
        return b

    def _release(self, items):
        for (b, (space, lo, hi)) in items:
            ev = dict(b.r)
            if b.w is not None and ev.get(b.w[0], 0) < b.w[1]:
                ev[b.w[0]] = b.w[1]
            self.freed = [f for f in self.freed if not (f[0] == space and lo <= f[1] and f[2] <= hi)]
            self.freed.append((space, lo, hi, ev))

    def dram(self, name, shape, dtype, kind="Internal"):
        t = self.nc.dram_tensor(name, list(shape), dtype, kind=kind)
        return Buf(t.ap(), name)

    def emit(self, eng, fn, reads=(), writes=()):
        E = self.eng[eng]
        S = self.order[self.stream[eng]]
        deps = []
        for b in reads:
            if b.w is not None:
                deps.append(b.w)
        for b in writes:
            if b.w is not None:
                deps.append(b.w)
            deps.extend(b.r.items())
        if E.is_dma:
            idx = E.n % self.NRING
            E.n += 1
            sem = E.sems[idx]
            if E.vals[idx] > 0:
                deps.append((sem, E.vals[idx]))
            E.vals[idx] += 16
            val = E.vals[idx]
            inc = 16
        else:
            E.count += 1
            sem = E.sem
            val = E.count
            inc = 1
        W = self._waited(eng)
        for (s, v) in deps:
            if E.is_pe and s == E.sem:
                continue
            if W.get(s, 0) < v:
                S.append(("wait", s, v))
                W[s] = v
        S.append(("op", fn, sem, inc))
        for b in reads:
            if b.r.get(sem, 0) < val:
                b.r[sem] = val
        for b in writes:
            b.w = (sem, val)
            b.r = {}
        return (sem, val)

    def _waited(self, eng):
        st = self.stream[eng]
        if not hasattr(self, "_w"):
            self._w = {k: {} for k in self.order}
        return self._w[st]

    def wait_all(self, eng, bufs):
        S = self.order[self.stream[eng]]
        W = self._waited(eng)
        for b in bufs:
            deps = []
            if b.w is not None:
                deps.append(b.w)
            deps.extend(b.r.items())
            for (s, v) in deps:
                if W.get(s, 0) < v:
                    S.append(("wait", s, v))
                    W[s] = v

    def barrier(self):
        evs = []
        for E in self.eng.values():
            if E.is_dma:
                evs.extend((s, v) for s, v in zip(E.sems, E.vals) if v > 0)
            elif E.count > 0:
                evs.append((E.sem, E.count))
        for st, S in self.order.items():
            W = self._w[st] if hasattr(self, "_w") else self._waited("pe") and self._w[st]
            for (s, v) in evs:
                if W.get(s, 0) < v:
                    S.append(("wait", s, v))
                    W[s] = v

    def phase(self):
        ctx = self

        class _P:
            def __enter__(s):
                s.saved = ctx.es
                ctx.es = ExitStack()
                ctx.live.append([])
                return s

            def __exit__(s, *a):
                ctx._release(ctx.live.pop())
                ctx.es.close()
                ctx.es = s.saved
                return False
        return _P()

    def finish(self):
        nc = self.nc
        semh = self.semh
        order = self.order

        def play(name):
            def run(e):
                for it in order[name]:
                    if it[0] == "wait":
                        e.wait_ge(semh[it[1]], it[2])
                    else:
                        it[1](e).then_inc(semh[it[2]], it[3])
            return run

        with nc.Block() as block:
            block.tensor(play("tensor"))
            block.scalar(play("scalar"))
            block.vector(play("vector"))
            block.gpsimd(play("gpsimd"))
            block.sync(play("sync"))
        self.es.close()

    def mm(self, out, lhsT, rhs, start=True, stop=True, reads=(), writes=()):
        return self.emit("pe", lambda e: e.matmul(out, lhsT=lhsT, rhs=rhs, start=start, stop=stop),
                         reads, writes)

    def tr(self, out, in_, ident, reads=(), writes=()):
        return self.emit("pe", lambda e: e.transpose(out, in_, ident), reads, writes)

    def act(self, out, in_, func, reads=(), writes=(), **kw):
        return self.emit("act", lambda e: e.activation(out=out, in_=in_, func=func, **kw), reads, writes)

    def tt(self, out, in0, in1, op, reads=(), writes=(), eng="dve"):
        return self.emit(eng, lambda e: e.tensor_tensor(out=out, in0=in0, in1=in1, op=op), reads, writes)

    def ts(self, out, in0, s1, s2, op0, op1=None, reads=(), writes=(), eng="dve", **kw):
        if op1 is None:
            return self.emit(eng, lambda e: e.tensor_scalar(out=out, in0=in0, scalar1=s1, scalar2=None,
                                                            op0=op0, **kw), reads, writes)
        return self.emit(eng, lambda e: e.tensor_scalar(out=out, in0=in0, scalar1=s1, scalar2=s2,
                                                        op0=op0, op1=op1, **kw), reads, writes)

    def stt(self, out, in0, scalar, in1, op0, op1, reads=(), writes=(), eng="dve"):
        return self.emit(eng, lambda e: e.scalar_tensor_tensor(out=out, in0=in0, scalar=scalar, in1=in1,
                                                               op0=op0, op1=op1), reads, writes)

    def copy(self, out, in_, reads=(), writes=(), eng="dve"):
        if eng == "act":
            return self.emit("act", lambda e: e.copy(out=out, in_=in_), reads, writes)
        return self.emit(eng, lambda e: e.tensor_copy(out=out, in_=in_), reads, writes)

    def dma(self, out, in_, reads=(), writes=(), q="sp", **kw):
        return self.emit(q, lambda e: e.dma_start(out=out, in_=in_, **kw), reads, writes)


D = 2048
KC = 16
HD = 128
AH, BH, CH = 6, 5, 5
AW, BW, CW = AH * HD, BH * HD, CH * HD
FF = 5632
IN_COLS = 13578
OFF_AQ, OFF_AK, OFF_AV = 0, 768, 1536
OFF_BQ, OFF_BK, OFF_BV, OFF_BG = 2304, 2944, 3584, 4224
OFF_CQKV, OFF_CZ, OFF_CBA, OFF_GATE = 4864, 6784, 7424, 7434
EPS = 1e-6
NT = 512
NEG = -30000.0
RET_LG = [float(np.log1p(-np.exp2(-5.0 - h))) for h in range(BH)]


def host_tables(T):
    f32 = np.float32
    tb = {}
    tb["ident"] = np.eye(128, dtype=f32)
    pos = np.arange(T, dtype=f32)
    rope_freq = (500000.0 ** (-np.arange(0, 32, 2, dtype=f32) / 32)).astype(f32)
    ang = pos[:, None] * rope_freq[None, :]
    tb["ropeA"] = np.concatenate([np.cos(ang), np.sin(ang)], axis=1).astype(f32)
    ret_freq = (10000.0 ** (-np.linspace(0.0, 1.0, 64, dtype=f32))).astype(f32)
    angb = pos[:, None] * ret_freq[None, :]
    tb["ropeB"] = np.concatenate([np.cos(angb), np.sin(angb)], axis=1).astype(f32)
    i = np.arange(128, dtype=np.float64)
    rel = i[None, :] - i[:, None]
    sc = 128.0 ** -0.5
    dec = np.zeros((128, BH, 128), f32)
    xi = np.zeros((128, BH, 128), f32)
    zeta = np.zeros((128, BH), f32)
    for h in range(BH):
        lg = RET_LG[h]
        dec[:, h, :] = np.where(rel >= 0, np.exp(lg * np.maximum(rel, 0)), 0.0) * sc
        xi[:, h, :] = np.exp(lg * (i + 1.0))[None, :]
        zeta[:, h] = np.exp(lg * (127 - i)) * sc
    tb["decT"] = dec.reshape(128, BH * 128)
    tb["xibc"] = xi.reshape(128, BH * 128)
    tb["zeta"] = zeta
    ut = (rel >= 0).astype(f32)
    tb["UT"] = ut
    tb["SUT"] = (rel > 0).astype(f32)
    tb["TRI"] = np.where(rel >= 0, 0.0, NEG).astype(f32)
    nblk = T // 256
    negm = np.zeros((nblk + 1, 16), f32)
    valid = np.zeros((nblk + 1, 16), f32)
    own = np.zeros((nblk + 1, 16), f32)
    for b in range(nblk + 1):
        for n in range(16):
            if n < b:
                valid[b, n] = 1.0
            else:
                negm[b, n] = -1e30
            if n == b:
                own[b, n] = 1.0
    tb["negm"] = negm.reshape(-1)
    tb["valid"] = valid.reshape(-1)
    tb["own"] = own.reshape(-1)
    e = np.zeros((16, 16, 128), f32)
    for n in range(16):
        e[n, n, :] = 1.0
    tb["E"] = e.reshape(16, 16 * 128)
    return tb


def build(T, DEPTH, phases=("A", "B", "C", "M", "F")):
    NTI = T // NT
    NSUBT = T // 128
    NBLK = T // 256
    nc = bass.Bass("TRN2", target_bir_lowering=False)
    c = Ctx(nc)

    def din(name, shape):
        return nc.dram_tensor(name, list(shape), F32, kind="ExternalInput").ap()

    x_d = din("x", [T, D])
    attn_norm = din("attn_norm", [DEPTH, D])
    w_in = din("w_in", [DEPTH, D, IN_COLS])
    q_norm = din("q_norm", [DEPTH, HD])
    k_norm = din("k_norm", [DEPTH, HD])
    ret_norm = din("ret_norm", [DEPTH, BW])
    conv_w = din("conv_w", [DEPTH, 4, 3 * CW])
    a_log = din("a_log", [DEPTH, CH])
    dt_bias = din("dt_bias", [DEPTH, CH])
    gdn_norm = din("gdn_norm", [DEPTH, HD])
    w_branch = din("w_branch", [DEPTH, D, D])
    w_out = din("w_out", [DEPTH, D, D])
    ffn_norm = din("ffn_norm", [DEPTH, D])
    w_gate = din("w_gate", [DEPTH, D, FF])
    w_up = din("w_up", [DEPTH, D, FF])
    w_down = din("w_down", [DEPTH, FF, D])
    tbs = host_tables(T)
    tb_d = {k: din("tb_" + k, v.shape) for k, v in tbs.items()}
    y_d = nc.dram_tensor("y", [T, D], F32, kind="ExternalOutput").ap()
    y_buf = Buf(y_d, "y")

    kth = [c.dram(f"kth{l}", [AH, 128, T], BF16) for l in range(DEPTH)]
    vh = [c.dram(f"vh{l}", [T, AW], BF16) for l in range(DEPTH)]

    xt = c.sbuf([128, 4, D], F32, "xt")
    hT = c.sbuf([128, KC, NT], BF16, "hT")
    OT = c.sbuf([128, KC, NT], BF16, "OT")
    NWB = 2
    wring = [c.sbuf([128, KC, 512], BF16, f"wr{i}") for i in range(NWB)]
    wstate = {"i": 0}
    identf = c.sbuf([128, 128], F32, "identf")
    identb = c.sbuf([128, 128], BF16, "identb")
    onesb = c.sbuf([128, 128], BF16, "onesb")
    onesf = c.sbuf([128, 128], F32, "onesf")
    UTf = c.sbuf([128, 128], F32, "UTf")
    SUTf = c.sbuf([128, 128], F32, "SUTf")
    TRIb = c.sbuf([128, 128], BF16, "TRIb")
    Eb = c.sbuf([16, 16 * 128], BF16, "Eb")
    decT = c.sbuf([128, BW], F32, "decT")
    xibc = c.sbuf([128, BW], F32, "xibc")
    zeta = c.sbuf([128, BH], F32, "zeta")
    negm = c.sbuf([128, (NBLK + 1) * 16], F32, "negm")
    valid = c.sbuf([128, (NBLK + 1) * 16], F32, "valid")
    own = c.sbuf([128, (NBLK + 1) * 16], F32, "own")
    small = c.sbuf([128, 64], F32, "small")
    kmT = [c.sbuf([128, AH, 16], BF16, f"kmT{l}") for l in range(DEPTH)]
    bstate = [c.sbuf([128, BW], F32, f"bst{l}") for l in range(DEPTH)]
    bstate_b = [c.sbuf([128, BW], BF16, f"bstb{l}") for l in range(DEPTH)]
    cstate = [c.sbuf([128, CW], F32, f"cst{l}") for l in range(DEPTH)]
    cstate_b = [c.sbuf([128, CW], BF16, f"cstb{l}") for l in range(DEPTH)]
    carry = [c.sbuf([128, 15, 3], F32, f"carry{l}") for l in range(DEPTH)]
    lvec = [c.sbuf([128, 3 * HD + BW + 2 * CH + 60], F32, f"lvec{l}") for l in range(DEPTH)]

    with c.phase():
        tmpf = c.sbuf([128, 2048], F32, "tmpf")
        c.dma(identf[:], tb_d["ident"], writes=[identf])
        c.copy(identb[:], identf[:], reads=[identf], writes=[identb])
        c.emit("dve", lambda e: e.memset(onesb[:], 1.0), writes=[onesb])
        c.emit("dve", lambda e: e.memset(onesf[:], 1.0), writes=[onesf])
        c.dma(UTf[:], tb_d["UT"], writes=[UTf])
        c.dma(SUTf[:], tb_d["SUT"], writes=[SUTf])
        c.dma(tmpf[:, 0:128], tb_d["TRI"], writes=[tmpf])
        c.copy(TRIb[:], tmpf[:, 0:128], reads=[tmpf], writes=[TRIb])
        c.dma(tmpf[0:16, :], tb_d["E"], writes=[tmpf])
        c.copy(Eb[:], tmpf[0:16, :], reads=[tmpf], writes=[Eb])
        c.dma(decT[:], tb_d["decT"], writes=[decT])
        c.dma(xibc[:], tb_d["xibc"], writes=[xibc])
        c.dma(zeta[:], tb_d["zeta"], writes=[zeta])
        c.dma(negm[:], tb_d["negm"].partition_broadcast(128), writes=[negm])
        c.dma(valid[:], tb_d["valid"].partition_broadcast(128), writes=[valid])
        c.dma(own[:], tb_d["own"].partition_broadcast(128), writes=[own])
        for l in range(DEPTH):
            lv = lvec[l]
            c.dma(lv[:, 0:128], q_norm[l].partition_broadcast(128), writes=[lv])
            c.dma(lv[:, 128:256], k_norm[l].partition_broadcast(128), writes=[lv])
            c.dma(lv[:, 256:384], gdn_norm[l].partition_broadcast(128), writes=[lv])
            c.dma(lv[:, 384:1024], ret_norm[l].partition_broadcast(128), writes=[lv])
            c.dma(lv[:, 1024:1029], a_log[l].partition_broadcast(128), writes=[lv])
            c.dma(lv[:, 1029:1034], dt_bias[l].partition_broadcast(128), writes=[lv])
            for j in range(4):
                c.dma(lv[:, 1034 + j * 15:1034 + (j + 1) * 15], conv_w[l][j].rearrange("(b p) -> p b", p=128),
                      writes=[lv], allow_slow_non_contiguous=True)
            c.act(lv[:, 1024:1029], lv[:, 1024:1029], AF.Exp, reads=[lv], writes=[lv])
            c.ts(lv[:, 1024:1029], lv[:, 1024:1029], -1.0, None, ALU.mult, reads=[lv], writes=[lv])
            c.emit("dve", lambda e, t=kmT[l]: e.memset(t[:], 0.0), writes=[kmT[l]])
            c.emit("dve", lambda e, t=bstate[l]: e.memset(t[:], 0.0), writes=[bstate[l]])
            c.emit("dve", lambda e, t=bstate_b[l]: e.memset(t[:], 0.0), writes=[bstate_b[l]])
            c.emit("dve", lambda e, t=cstate[l]: e.memset(t[:], 0.0), writes=[cstate[l]])
            c.emit("dve", lambda e, t=cstate_b[l]: e.memset(t[:], 0.0), writes=[cstate_b[l]])
            c.emit("dve", lambda e, t=carry[l]: e.memset(t[:], 0.0), writes=[carry[l]])

    def wload(ap, nk, ncols):
        b = wring[wstate["i"] % NWB]
        wstate["i"] += 1
        c.dma(b[:, 0:nk, 0:ncols], ap.rearrange("(k p) n -> p k n", p=128), writes=[b], q="poolq")
        return b

    evac_rr = {"i": 0}

    def evac_eng():
        evac_rr["i"] += 1
        return "act" if evac_rr["i"] % 2 else "dve"

    def rmsnorm_to_hT(gain_ap, pst):
        gain = c.sbuf([128, D], F32, "gain")
        c.dma(gain[:], gain_ap.partition_broadcast(128), writes=[gain])
        ssq = c.sbuf([128, 8], F32)
        junk = c.sbuf([128, D], BF16)
        hb = [c.sbuf([128, D], BF16) for _ in range(2)]
        c.emit("dve", lambda e: e.memset(ssq[:], 0.0), writes=[ssq])
        for s in range(4):
            c.act(junk[:], xt[:, s, :], AF.Square, reads=[xt], writes=[junk, ssq], accum_out=ssq[:, s:s + 1])
        c.act(ssq[:, 4:8], ssq[:, 0:4], AF.Sqrt, reads=[ssq], writes=[ssq], scale=1.0 / D, bias=EPS)
        c.emit("dve", lambda e: e.reciprocal(out=ssq[:, 4:8], in_=ssq[:, 4:8]), reads=[ssq], writes=[ssq])
        for s in range(4):
            h = hb[s % 2]
            c.stt(h[:], xt[:, s, :], ssq[:, 4 + s:5 + s], gain[:], ALU.mult, ALU.mult,
                  reads=[xt, ssq, gain], writes=[h])
            for g in range(2):
                pt = pst[(s * 2 + g) % len(pst)]
                for j in range(8):
                    kc = g * 8 + j
                    c.tr(pt[:, j * 128:(j + 1) * 128], h[:, kc * 128:(kc + 1) * 128], identb[:],
                         reads=[h, identb], writes=[pt])
                c.copy(hT[:, g * 8:(g + 1) * 8, s * 128:(s + 1) * 128],
                       pt[:, 0:1024].rearrange("p (a b) -> p a b", a=8), reads=[pt], writes=[hT],
                       eng=evac_eng())

    def proj_tm(wap, ncols, psr, consume):
        wb = wload(wap, KC, ncols)
        for s in range(4):
            ps = psr[proj_tm.i % len(psr)]
            proj_tm.i += 1
            for kc in range(KC):
                c.mm(ps[:, 0:ncols], hT[:, kc, s * 128:(s + 1) * 128], wb[:, kc, 0:ncols],
                     start=(kc == 0), stop=(kc == KC - 1), reads=[hT, wb], writes=[ps])
            consume(s, ps)
    proj_tm.i = 0

    def proj_fm(wap, ncols, psr, consume, src=None, nk=KC):
        src = src or hT
        wb = wload(wap, nk, ncols)
        for j in range(ncols // 128):
            ps = psr[proj_tm.i % len(psr)]
            proj_tm.i += 1
            for kc in range(nk):
                c.mm(ps[:, 0:NT], wb[:, kc, j * 128:(j + 1) * 128], src[:, kc, :],
                     start=(kc == 0), stop=(kc == nk - 1), reads=[src, wb], writes=[ps])
            consume(j, ps)

    def bc_h(ap2d, nh):
        return ap2d[:, None, :].broadcast_to([128, nh, ap2d.shape[1]])

    def bc_d(ap2d, d=128):
        return ap2d[:, :, None].broadcast_to([128, ap2d.shape[1], d])

    def v3(ap, nh):
        return ap.rearrange("p (h d) -> p h d", h=nh)

    def rstd_from_ssq(dst, ssq_ap, scale, reads, writes, post_mul=None):
        c.act(dst, ssq_ap, AF.Sqrt, reads=reads, writes=writes, scale=scale, bias=EPS)
        c.emit("dve", lambda e: e.reciprocal(out=dst, in_=dst), reads=writes, writes=writes)
        if post_mul is not None:
            c.ts(dst, dst, post_mul, None, ALU.mult, reads=writes, writes=writes)

    def rotary_full(dst_bf, src_f, cs, nh, tmp):
        x1, x2 = src_f[:, :, 0:64], src_f[:, :, 64:128]
        cos = cs[:, None, 0:64].broadcast_to([128, nh, 64])
        sin = cs[:, None, 64:128].broadcast_to([128, nh, 64])
        t1, t2 = tmp
        c.tt(v3(t1[:, 0:nh * 64], nh), x1, cos, ALU.mult, reads=[srcbuf[0], csbuf[0]], writes=[t1])
        c.tt(v3(t2[:, 0:nh * 64], nh), x2, sin, ALU.mult, reads=[srcbuf[0], csbuf[0]], writes=[t2])
        c.tt(dst_bf[:, :, 0:64], v3(t1[:, 0:nh * 64], nh), v3(t2[:, 0:nh * 64], nh), ALU.subtract,
             reads=[t1, t2], writes=[dstbuf[0]])
        c.tt(v3(t1[:, 0:nh * 64], nh), x2, cos, ALU.mult, reads=[srcbuf[0], csbuf[0], dstbuf[0]], writes=[t1])
        c.tt(v3(t2[:, 0:nh * 64], nh), x1, sin, ALU.mult, reads=[srcbuf[0], csbuf[0]], writes=[t2])
        c.tt(dst_bf[:, :, 64:128], v3(t1[:, 0:nh * 64], nh), v3(t2[:, 0:nh * 64], nh), ALU.add,
             reads=[t1, t2], writes=[dstbuf[0]])
    srcbuf, csbuf, dstbuf = [None], [None], [None]

    def out_norm_gate(ops, gain_ap3, gate_ap3, obf, tmpA, tmpB, ssq5):
        og = tmpA
        c.copy(og[:, 0:640], ops[:, 0:640], reads=[ops], writes=[og], eng="act")
        c.tt(tmpB[:, 0:640], og[:, 0:640], og[:, 0:640], ALU.mult, reads=[og], writes=[tmpB])
        c.emit("dve", lambda e: e.reduce_sum(out=ssq5[:, 0:5], in_=v3(tmpB[:, 0:640], 5), axis=AX.X),
               reads=[tmpB], writes=[ssq5])
        rstd_from_ssq(ssq5[:, 0:5], ssq5[:, 0:5], 1.0 / 128, [ssq5], [ssq5])
        c.tt(v3(og[:, 0:640], 5), v3(og[:, 0:640], 5), bc_d(ssq5[:, 0:5]), ALU.mult, reads=[og, ssq5], writes=[og])
        c.tt(v3(og[:, 0:640], 5), v3(og[:, 0:640], 5), gain_ap3, ALU.mult, reads=[og] + gainbuf, writes=[og])
        c.tt(v3(obf[:, 0:640], 5), v3(og[:, 0:640], 5), gate_ap3, ALU.mult, reads=[og] + gatebuf, writes=[obf])
    gainbuf, gatebuf = [], []

    def phase_B(l, ti):
        lv = lvec[l]
        win = w_in[l]
        t0 = ti * NT
        st_f, st_b = bstate[l], bstate_b[l]
        with c.phase():
            qf = c.sbuf([128, 4, BW], BF16, "Bqf")
            kf = c.sbuf([128, 4, BW], BF16, "Bkf")
            vt = c.sbuf([128, 4, BW], BF16, "Bvt")
            gs = c.sbuf([128, 4, BW], BF16, "Bgs")
            cs = c.sbuf([128, 4, 128], F32, "Bcs")
            c.dma(cs[:], tb_d["ropeB"][t0:t0 + NT, :].rearrange("(s p) d -> p s d", p=128), writes=[cs])
            with c.phase():
                psr = [c.psum([128, 512], F32) for _ in range(4)]
                for (dst, off, kind) in ((qf, OFF_BQ, "f"), (kf, OFF_BK, "f"), (vt, OFF_BV, "b"), (gs, OFF_BG, "s")):
                    for (c0, nco) in ((0, 512), (512, 128)):
                        def cons(s, ps, dst=dst, c0=c0, nco=nco, kind=kind):
                            if kind == "s":
                                c.act(dst[:, s, c0:c0 + nco], ps[:, 0:nco], AF.Silu, reads=[ps], writes=[dst])
                            else:
                                c.copy(dst[:, s, c0:c0 + nco], ps[:, 0:nco], reads=[ps], writes=[dst], eng=evac_eng())
                        proj_tm(win[:, off + c0:off + c0 + nco], nco, psr, cons)
            with c.phase():
                qr = c.sbuf([128, 4, BW], BF16, "Bqr")
                kr = c.sbuf([128, 4, BW], BF16, "Bkr")
                kz = c.sbuf([128, 4, BW], BF16, "Bkz")
                QT = c.sbuf([128, BH, NT], BF16, "BQT")
                KT = c.sbuf([128, BH, NT], BF16, "BKT")
                QX = c.sbuf([128, BH, NT], BF16, "BQX")
                t1 = c.sbuf([128, 640], F32, "Bt1")
                t2 = c.sbuf([128, 640], F32, "Bt2")
                scT = c.sbuf([128, BW], BF16, "BscT")
                obf = c.sbuf([128, BW], BF16, "Bobf")
                ssq5 = c.sbuf([128, 8], F32, "Bssq")
                ptb = [c.psum([128, 1024], BF16) for _ in range(2)]
                pA = c.psum([128, 1024], F32, "BpA")
                pB = c.psum([128, 1024], F32, "BpB")
                pC = c.psum([128, 1024], F32, "BpC")
                for s in range(4):
                    for (src, dst) in ((qf, qr), (kf, kr)):
                        srcbuf[0], csbuf[0], dstbuf[0] = src, cs, dst
                        rotary_full(v3(dst[:, s, :], BH), v3(src[:, s, :], BH), cs[:, s, :], BH, (t1, t2))
                    c.tt(v3(kz[:, s, :], BH), v3(kr[:, s, :], BH), bc_d(zeta[:, 0:BH]), ALU.mult,
                         reads=[kr, zeta], writes=[kz])
                    for (src, dst, pt) in ((qr, QT, ptb[0]), (kr, KT, ptb[1])):
                        for h in range(BH):
                            c.tr(pt[:, h * 128:(h + 1) * 128], src[:, s, h * 128:(h + 1) * 128], identb[:],
                                 reads=[src, identb], writes=[pt])
                        c.copy(dst[:, :, s * 128:(s + 1) * 128], v3(pt[:, 0:640], BH), reads=[pt], writes=[dst],
                               eng=evac_eng())
                c.tt(QX[:].rearrange("p h (s t) -> p h s t", s=4), QT[:].rearrange("p h (s t) -> p h s t", s=4),
                     v3(xibc[:], BH)[:, :, None, :].broadcast_to([128, BH, 4, 128]), ALU.mult,
                     reads=[QT, xibc], writes=[QX])
                for s in range(4):
                    sl = slice(s * 128, (s + 1) * 128)
                    for h in range(BH):
                        c.mm(pA[:, h * 128:(h + 1) * 128], KT[:, h, sl], QT[:, h, sl], reads=[KT, QT], writes=[pA])
                    c.tt(scT[:], pA[:, 0:640], decT[:], ALU.mult, reads=[pA, decT], writes=[scT])
                    for h in range(BH):
                        hs = slice(h * 128, (h + 1) * 128)
                        c.mm(pB[:, hs], scT[:, hs], vt[:, s, hs], start=True, stop=False, reads=[scT, vt], writes=[pB])
                        c.mm(pB[:, hs], QX[:, h, sl], st_b[:, hs], start=False, stop=True, reads=[QX, st_b], writes=[pB])
                    for h in range(BH):
                        hs = slice(h * 128, (h + 1) * 128)
                        c.mm(pC[:, hs], kz[:, s, hs], vt[:, s, hs], reads=[kz, vt], writes=[pC])
                    for h in range(BH):
                        hs = slice(h * 128, (h + 1) * 128)
                        c.stt(st_f[:, hs], st_f[:, hs], float(np.exp(RET_LG[h] * 128)), pC[:, hs], ALU.mult, ALU.add,
                              reads=[st_f, pC], writes=[st_f])
                    c.copy(st_b[:], st_f[:], reads=[st_f], writes=[st_b], eng="act")
                    gainbuf[:] = [lv]
                    gatebuf[:] = [gs]
                    out_norm_gate(pB, v3(lv[:, 384:1024], BH), v3(gs[:, s, :], BH), obf, t1, t2, ssq5)
                    pt = ptb[s % 2]
                    for h in range(BH):
                        c.tr(pt[:, h * 128:(h + 1) * 128], obf[:, h * 128:(h + 1) * 128], identb[:],
                             reads=[obf, identb], writes=[pt])
                    c.copy(OT[:, 6:11, sl], v3(pt[:, 0:640], BH), reads=[pt], writes=[OT], eng=evac_eng())

    def phase_A(l, ti):
        lv = lvec[l]
        win = w_in[l]
        t0 = ti * NT
        sc = 128.0 ** -0.5
        with c.phase():
            vt = c.sbuf([128, 4, AW], BF16, "Avt")
            QT = c.sbuf([128, AH, NT], BF16, "AQT")
            KT = c.sbuf([128, AH, NT], BF16, "AKT")
            biasT = c.sbuf([16, AH, NT], BF16, "AbiasT")
            cs = c.sbuf([128, 4, 32], F32, "Acs")
            c.dma(cs[:], tb_d["ropeA"][t0:t0 + NT, :].rearrange("(s p) d -> p s d", p=128), writes=[cs])
            with c.phase():
              qf = c.sbuf([128, 4, AW], F32, "Aqf")
              kf = c.sbuf([128, 4, AW], F32, "Akf")
              if True:
                with c.phase():
                    psr = [c.psum([128, 512], F32) for _ in range(4)]
                    for (dst, off) in ((qf, OFF_AQ), (kf, OFF_AK), (vt, OFF_AV)):
                        for (c0, nco) in ((0, 512), (512, 256)):
                            def cons(s, ps, dst=dst, c0=c0, nco=nco):
                                c.copy(dst[:, s, c0:c0 + nco], ps[:, 0:nco], reads=[ps], writes=[dst], eng=evac_eng())
                            proj_tm(win[:, off + c0:off + c0 + nco], nco, psr, cons)
                c.dma(vh[l][t0:t0 + NT, :].rearrange("(s p) d -> p s d", p=128), vt[:], reads=[vt], writes=[vh[l]])
                with c.phase():
                    tmp = c.sbuf([128, AW], F32, "Atmp")
                    ssq = c.sbuf([128, 8], F32, "Assq")
                    rt = [c.sbuf([128, AH * 16], F32, f"Art{i}") for i in range(4)]
                    qb = c.sbuf([128, AW], BF16, "Aqb")
                    ptb = [c.psum([128, 1024], BF16) for _ in range(2)]
                    pg = c.psum([128, 512], F32, "Apg")
                    gm = c.sbuf([128, AH * 16], F32, "Agm")
                    m8 = c.sbuf([128, AH * 8], F32, "Am8")
                    bb = c.sbuf([128, AH * 16], BF16, "Abb")
                    kmf = c.sbuf([128, AH * 2], F32, "Akmf")
                    for s in range(4):
                        sl = slice(s * 128, (s + 1) * 128)
                        for (src, dst, g0, post) in ((kf, KT, 128, None), (qf, QT, 0, sc)):
                            x3 = v3(src[:, s, :], AH)
                            c.tt(tmp[:], src[:, s, :], src[:, s, :], ALU.mult, reads=[src], writes=[tmp])
                            c.emit("dve", lambda e: e.reduce_sum(out=ssq[:, 0:AH], in_=v3(tmp[:], AH), axis=AX.X),
                                   reads=[tmp], writes=[ssq])
                            rstd_from_ssq(ssq[:, 0:AH], ssq[:, 0:AH], 1.0 / 128, [ssq], [ssq], post_mul=post)
                            c.tt(x3, x3, bc_d(ssq[:, 0:AH]), ALU.mult, reads=[src, ssq], writes=[src])
                            c.tt(x3, x3, bc_h(lv[:, g0:g0 + 128], AH), ALU.mult, reads=[src, lv], writes=[src])
                            x1, x2 = x3[:, :, 0:16], x3[:, :, 16:32]
                            cos = cs[:, s, None, 0:16].broadcast_to([128, AH, 16])
                            sin = cs[:, s, None, 16:32].broadcast_to([128, AH, 16])
                            r = [v3(t[:], AH) for t in rt]
                            c.tt(r[0], x1, cos, ALU.mult, reads=[src, cs], writes=[rt[0]])
                            c.tt(r[1], x2, sin, ALU.mult, reads=[src, cs], writes=[rt[1]])
                            c.tt(r[2], x2, cos, ALU.mult, reads=[src, cs], writes=[rt[2]])
                            c.tt(r[3], x1, sin, ALU.mult, reads=[src, cs], writes=[rt[3]])
                            q3 = v3(qb[:], AH)
                            c.tt(q3[:, :, 0:16], r[0], r[1], ALU.subtract, reads=[rt[0], rt[1]], writes=[qb])
                            c.tt(q3[:, :, 16:32], r[2], r[3], ALU.add, reads=[rt[2], rt[3]], writes=[qb])
                            c.copy(q3[:, :, 32:128], x3[:, :, 32:128], reads=[src], writes=[qb], eng="act")
                            pt = ptb[0] if dst is KT else ptb[1]
                            for h in range(AH):
                                c.tr(pt[:, h * 128:(h + 1) * 128], qb[:, h * 128:(h + 1) * 128], identb[:],
                                     reads=[qb, identb], writes=[pt])
                            c.copy(dst[:, :, sl], v3(pt[:, 0:768], AH), reads=[pt], writes=[dst], eng=evac_eng())
                    c.emit("dve", lambda e: e.reduce_sum(out=kmf[:].rearrange("p (h b) -> p h b", h=AH),
                                                         in_=KT[:].rearrange("p h (b t) -> p h b t", b=2), axis=AX.X),
                           reads=[KT], writes=[kmf])
                    c.ts(kmT[l][:, :, 2 * ti:2 * ti + 2], kmf[:].rearrange("p (h b) -> p h b", h=AH), 1.0 / 256, None,
                         ALU.mult, reads=[kmf], writes=[kmT[l]])
                    c.dma(kth[l][:, :, t0:t0 + NT].rearrange("h p t -> p h t"), KT[:], reads=[KT], writes=[kth[l]])
                    for s in range(4):
                        sl = slice(s * 128, (s + 1) * 128)
                        blk = 2 * ti + s // 2
                        bsl = slice(blk * 16, (blk + 1) * 16)
                        for h in range(AH):
                            c.mm(pg[:, h * 16:(h + 1) * 16], QT[:, h, sl], kmT[l][:, h, :], reads=[QT, kmT[l]], writes=[pg])
                        g3 = v3(gm[:], AH)
                        c.tt(g3, v3(pg[:, 0:AH * 16], AH), bc_h(negm[:, bsl], AH), ALU.add, reads=[pg, negm], writes=[gm])
                        for h in range(AH):
                            c.emit("dve", lambda e, h=h: e.max(out=m8[:, h * 8:(h + 1) * 8], in_=gm[:, h * 16:(h + 1) * 16]),
                                   reads=[gm], writes=[m8])
                        c.tt(g3, g3, v3(m8[:], AH)[:, :, 2:3].broadcast_to([128, AH, 16]), ALU.is_ge,
                             reads=[gm, m8], writes=[gm])
                        c.tt(g3, g3, bc_h(valid[:, bsl], AH), ALU.mult, reads=[gm, valid], writes=[gm])
                        c.tt(g3, g3, bc_h(own[:, bsl], AH), ALU.add, reads=[gm, own], writes=[gm])
                        c.ts(bb[:], gm[:], -NEG, NEG, ALU.mult, ALU.add, reads=[gm], writes=[bb])
                        pt = ptb[s % 2]
                        for h in range(AH):
                            c.tr(pt[0:16, h * 128:(h + 1) * 128], bb[:, h * 16:(h + 1) * 16], identb[:],
                                 reads=[bb, identb], writes=[pt])
                        c.copy(biasT[:, :, sl], v3(pt[0:16, 0:768], AH), reads=[pt], writes=[biasT], eng=evac_eng())
            with c.phase():
                npast = 4 * ti
                KH = [c.sbuf([128, max(npast, 1) * 128], BF16, f"AKH{i}") for i in range(2)]
                VH = [c.sbuf([128, max(npast, 1), 128], BF16, f"AVH{i}") for i in range(2)]
                PT = [c.sbuf([128, NT], BF16, f"APT{i}") for i in range(3)]
                rden = c.sbuf([128, NT], F32, "Arden")
                pST = [c.psum([128, 512], F32, f"ApST{i}") for i in range(3)]
                pO = [c.psum([128, 512], F32, f"ApO{i}") for i in range(2)]
                pD = [c.psum([128, 512], F32, f"ApD{i}") for i in range(2)]
                it = 0
                for h in range(AH):
                    kh, vhh = KH[h % 2], VH[h % 2]
                    if npast:
                        c.dma(kh[:, 0:npast * 128], kth[l][h, :, 0:npast * 128], reads=[kth[l]], writes=[kh])
                        c.dma(vhh[:, 0:npast, :],
                              vh[l][0:npast * 128, h * 128:(h + 1) * 128].rearrange("(k p) d -> p k d", p=128),
                              reads=[vh[l]], writes=[vhh])
                    po, pd = pO[h % 2], pD[h % 2]
                    nkt = npast + 4
                    for kt in range(nkt):
                        j = kt - npast
                        n = kt // 2
                        st = pST[it % 3]
                        pt_ = PT[it % 3]
                        it += 1
                        if j < 0:
                            klhs, vk, q0 = kh[:, kt * 128:(kt + 1) * 128], vhh[:, kt, :], 0
                            kr_, vr_ = [kh], [vhh]
                            c.mm(st[:, 0:NT], klhs, QT[:, h, :], start=True, stop=False, reads=kr_ + [QT], writes=[st])
                            c.mm(st[:, 0:NT], Eb[:, n * 128:(n + 1) * 128], biasT[:, h, :], start=False, stop=True,
                                 reads=[Eb, biasT], writes=[st])
                        else:
                            klhs, vk, q0 = KT[:, h, j * 128:(j + 1) * 128], vt[:, j, h * 128:(h + 1) * 128], j * 128
                            kr_, vr_ = [KT], [vt]
                            usebias = j < 2
                            c.mm(st[:, q0:q0 + 128], klhs, QT[:, h, q0:q0 + 128], start=True, stop=False,
                                 reads=kr_ + [QT], writes=[st])
                            if usebias:
                                c.mm(st[:, q0:q0 + 128], Eb[:, n * 128:(n + 1) * 128], biasT[:, h, q0:q0 + 128],
                                     start=False, stop=False, reads=[Eb, biasT], writes=[st])
                            c.mm(st[:, q0:q0 + 128], identb[:], TRIb[:], start=False, stop=True,
                                 reads=[identb, TRIb], writes=[st])
                            if q0 + 128 < NT:
                                c.mm(st[:, q0 + 128:NT], klhs, QT[:, h, q0 + 128:NT], start=True, stop=not usebias,
                                     reads=kr_ + [QT], writes=[st])
                                if usebias:
                                    c.mm(st[:, q0 + 128:NT], Eb[:, n * 128:(n + 1) * 128], biasT[:, h, q0 + 128:NT],
                                         start=False, stop=True, reads=[Eb, biasT], writes=[st])
                        c.act(pt_[:, q0:NT], st[:, q0:NT], AF.Exp, reads=[st], writes=[pt_])
                        c.mm(po[:, q0:NT], vk, pt_[:, q0:NT], start=(kt == 0), stop=(kt == nkt - 1),
                             reads=vr_ + [pt_], writes=[po])
                        c.mm(pd[:, q0:NT], onesb[:], pt_[:, q0:NT], start=(kt == 0), stop=(kt == nkt - 1),
                             reads=[onesb, pt_], writes=[pd])
                    c.emit("dve", lambda e, pd=pd: e.reciprocal(out=rden[:], in_=pd[:, 0:NT]), reads=[pd], writes=[rden])
                    c.tt(OT[:, h, :], po[:, 0:NT], rden[:], ALU.mult, reads=[po, rden], writes=[OT])

    def phase_C(l, ti):
        lv = lvec[l]
        win = w_in[l]
        t0 = ti * NT
        sc = 128.0 ** -0.5
        Sf, Sb = cstate[l], cstate_b[l]
        cwo = 1034
        with c.phase():
            qkvT = c.sbuf([128, 15, NT], BF16, "CqkvT")
            zs = c.sbuf([128, 4, CW], BF16, "Czs")
            ba = c.sbuf([128, 4, 16], F32, "Cba")
            beta = c.sbuf([128, 4, CH], F32, "Cbeta")
            nbeta = c.sbuf([128, 4, CH], F32, "Cnbeta")
            gg = c.sbuf([128, 4, CH], F32, "Cgg")
            QTc = c.sbuf([128, CH, NT], BF16, "CQT")
            KTc = c.sbuf([128, CH, NT], BF16, "CKT")
            Ktm = c.sbuf([128, 4, CW], BF16, "CKtm")
            Vtm = c.sbuf([128, 4, CW], BF16, "CVtm")
            with c.phase():
                psr = [c.psum([128, 512], F32) for _ in range(4)]
                xcb = [c.sbuf([128, NT + 3], F32, f"Cxcb{i}") for i in range(2)]
                acc = [c.sbuf([128, NT], F32, f"Cacc{i}") for i in range(2)]
                for (c0, nco) in ((0, 512), (512, 512), (1024, 512), (1536, 384)):
                    def cons(j, ps, c0=c0):
                        b = c0 // 128 + j
                        xb, ac = xcb[b % 2], acc[b % 2]
                        c.copy(xb[:, 3:NT + 3], ps[:, 0:NT], reads=[ps], writes=[xb], eng="act")
                        c.copy(xb[:, 0:3], carry[l][:, b, :], reads=[carry[l]], writes=[xb])
                        c.copy(carry[l][:, b, :], xb[:, NT:NT + 3], reads=[xb], writes=[carry[l]])
                        c.ts(ac[:], xb[:, 3:NT + 3], lv[:, cwo + 3 * 15 + b:cwo + 3 * 15 + b + 1], None, ALU.mult,
                             reads=[xb, lv], writes=[ac])
                        for jj in (2, 1, 0):
                            c.stt(ac[:], xb[:, jj:jj + NT], lv[:, cwo + jj * 15 + b:cwo + jj * 15 + b + 1], ac[:],
                                  ALU.mult, ALU.add, reads=[xb, lv, ac], writes=[ac])
                        c.act(qkvT[:, b, :], ac[:], AF.Silu, reads=[ac], writes=[qkvT])
                    proj_fm(win[:, OFF_CQKV + c0:OFF_CQKV + c0 + nco], nco, psr, cons)
                for (c0, nco) in ((0, 512), (512, 128)):
                    def consz(s, ps, c0=c0, nco=nco):
                        c.act(zs[:, s, c0:c0 + nco], ps[:, 0:nco], AF.Silu, reads=[ps], writes=[zs])
                    proj_tm(win[:, OFF_CZ + c0:OFF_CZ + c0 + nco], nco, psr, consz)

                def consb(s, ps):
                    c.copy(ba[:, s, 0:10], ps[:, 0:10], reads=[ps], writes=[ba])
                proj_tm(win[:, OFF_CBA:OFF_CBA + 10], 10, psr, consb)
                c.act(beta[:], ba[:, :, 0:5], AF.Sigmoid, reads=[ba], writes=[beta])
                c.ts(nbeta[:], beta[:], -1.0, None, ALU.mult, reads=[beta], writes=[nbeta])
                c.tt(gg[:], ba[:, :, 5:10], lv[:, None, 1029:1034].broadcast_to([128, 4, CH]), ALU.add,
                     reads=[ba, lv], writes=[gg])
                c.act(gg[:], gg[:], AF.Exp, reads=[gg], writes=[gg])
                c.act(gg[:], gg[:], AF.Ln, reads=[gg], writes=[gg], bias=1.0)
                c.tt(gg[:], gg[:], lv[:, None, 1024:1029].broadcast_to([128, 4, CH]), ALU.mult,
                     reads=[gg, lv], writes=[gg])
                sq = [c.sbuf([128, NT], BF16, f"Csq{i}") for i in range(2)]
                rs = [c.sbuf([128, NT], F32, f"Crs{i}") for i in range(2)]
                for b in range(10):
                    sq_, rs_ = sq[b % 2], rs[b % 2]
                    ps = psr[b % 4]
                    c.tt(sq_[:], qkvT[:, b, :], qkvT[:, b, :], ALU.mult, reads=[qkvT], writes=[sq_])
                    c.mm(ps[:, 0:NT], onesb[:], sq_[:], reads=[onesb, sq_], writes=[ps])
                    c.act(rs_[:], ps[:, 0:NT], AF.Sqrt, reads=[ps], writes=[rs_], scale=1.0, bias=EPS)
                    c.emit("dve", lambda e, rs_=rs_: e.reciprocal(out=rs_[:], in_=rs_[:]), reads=[rs_], writes=[rs_])
                    if b < 5:
                        c.stt(QTc[:, b, :], qkvT[:, b, :], sc, rs_[:], ALU.mult, ALU.mult, reads=[qkvT, rs_], writes=[QTc])
                    else:
                        c.tt(KTc[:, b - 5, :], qkvT[:, b, :], rs_[:], ALU.mult, reads=[qkvT, rs_], writes=[KTc])
            with c.phase():
                ptb = [c.psum([128, 1024], BF16) for _ in range(2)]
                for s in range(4):
                    sl = slice(s * 128, (s + 1) * 128)
                    for (src, b0, dst, pt) in ((KTc, 0, Ktm, ptb[0]), (qkvT, 10, Vtm, ptb[1])):
                        for h in range(CH):
                            c.tr(pt[:, h * 128:(h + 1) * 128], src[:, b0 + h, sl], identb[:], reads=[src, identb], writes=[pt])
                        c.copy(dst[:, s, :], pt[:, 0:640], reads=[pt], writes=[dst], eng=evac_eng())
            with c.phase():
                P2 = [c.psum([128, 1024], F32, f"CP{i}") for i in range(3)]
                ptb = c.psum([128, 1024], BF16, "Cptb")
                psm = c.psum([128, 512], F32, "Cpsm")
                pi = {"i": 0}

                def nps():
                    pi["i"] += 1
                    return P2[pi["i"] % 3]
                f = lambda n: c.sbuf([128, CW], F32, n)
                b16 = lambda n: c.sbuf([128, CW], BF16, n)
                GU, Dm, LT, LTs, Mf = f("CGU"), f("CDm"), f("CLT"), f("CLTs"), c.sbuf([128, CW], mybir.dt.float32r, "CMf")
                egb = GU
                og, t2 = LT, LTs
                fr = lambda n: c.sbuf([128, CW], mybir.dt.float32r, n)
                Nb, NTb, Pb, PTb = fr("CNb"), fr("CNTb"), fr("CPb"), fr("CPTb")
                Mb, attn, Qd, X, vn, KD, obf = (b16("CMb"), b16("Cattn"), b16("CQd"), b16("CX"), b16("Cvn"), b16("CKD"), b16("Cobf"))
                ssqC = c.sbuf([128, 8], F32, "CssqC")
                sm = c.sbuf([128, 32], F32, "Csm")
                for s in range(4):
                    sl = slice(s * 128, (s + 1) * 128)
                    g_s = gg[:, s, :]
                    c.tt(v3(GU[:], CH), bc_h(UTf[:], CH), bc_d(g_s), ALU.mult, reads=[UTf, gg], writes=[GU])
                    pg = nps()
                    for h in range(CH):
                        c.mm(pg[:, h * 128:(h + 1) * 128], onesf[:], GU[:, h * 128:(h + 1) * 128], reads=[onesf, GU], writes=[pg])
                    c.mm(psm[:, 0:5], UTf[:], g_s, reads=[UTf, gg], writes=[psm])
                    c.copy(sm[:, 0:5], psm[:, 0:5], reads=[psm], writes=[sm])
                    for h in range(CH):
                        hs = slice(h * 128, (h + 1) * 128)
                        c.ts(Dm[:, hs], pg[:, hs], sm[:, h:h + 1], 0.0, ALU.subtract, ALU.min, reads=[pg, sm], writes=[Dm])
                    c.act(LT[:], Dm[:], AF.Exp, reads=[Dm], writes=[LT])
                    c.tt(v3(LTs[:], CH), v3(LT[:], CH), bc_h(SUTf[:], CH), ALU.mult, reads=[LT, SUTf], writes=[LTs])
                    c.tt(v3(LT[:], CH), v3(LT[:], CH), bc_h(UTf[:], CH), ALU.mult, reads=[LT, UTf], writes=[LT])
                    c.act(egb[:], pg[:, 0:640], AF.Exp, reads=[pg], writes=[egb])
                    c.act(sm[:, 5:10], sm[:, 0:5], AF.Exp, reads=[sm], writes=[sm])
                    c.ts(sm[:, 5:10], sm[:, 5:10], -1.0, None, ALU.mult, reads=[sm], writes=[sm])
                    glast = v3(pg[:, 0:640], CH)[:, :, 127]
                    c.tt(sm[:, 10:15], glast, sm[:, 0:5], ALU.subtract, reads=[pg, sm], writes=[sm])
                    c.act(sm[:, 10:15], sm[:, 10:15], AF.Exp, reads=[sm], writes=[sm])
                    c.act(sm[:, 15:20], glast, AF.Exp, reads=[pg], writes=[sm])
                    pkk, pqk = nps(), nps()
                    for h in range(CH):
                        hs = slice(h * 128, (h + 1) * 128)
                        c.mm(pkk[:, hs], KTc[:, h, sl], KTc[:, h, sl], reads=[KTc], writes=[pkk])
                        c.mm(pqk[:, hs], KTc[:, h, sl], QTc[:, h, sl], reads=[KTc, QTc], writes=[pqk])
                    for h in range(CH):
                        hs = slice(h * 128, (h + 1) * 128)
                        c.stt(Nb[:, hs], pkk[:, hs], nbeta[:, s, h:h + 1], LTs[:, hs], ALU.mult, ALU.mult,
                              reads=[pkk, nbeta, LTs], writes=[Nb])
                    c.tt(attn[:], pqk[:, 0:640], LT[:], ALU.mult, reads=[pqk, LT], writes=[attn])
                    c.tt(v3(Qd[:], CH), QTc[:, :, sl], v3(egb[:], CH), ALU.mult, reads=[QTc, egb], writes=[Qd])
                    c.tt(v3(KD[:], CH), v3(Ktm[:, s, :], CH), bc_d(sm[:, 10:15]), ALU.mult, reads=[Ktm, sm], writes=[KD])
                    R_ = lambda ap: ap
                    pnt = nps()
                    for h in range(CH):
                        c.tr(pnt[:, h * 128:(h + 1) * 128], Nb[:, h * 128:(h + 1) * 128].bitcast(F32), identf[:], reads=[Nb, identf], writes=[pnt])
                    c.copy(R_(NTb[:]), pnt[:, 0:640], reads=[pnt], writes=[NTb], eng="act")
                    c.tt(R_(v3(Mf[:], CH)), v3(Nb[:], CH), bc_h(identf[:], CH), ALU.add, reads=[Nb, identf], writes=[Mf])
                    pairs = [(Nb, NTb), (Pb, PTb)]

                    def sq(k):
                        Pc, PTc = pairs[k % 2]
                        Pn, PTn = pairs[(k + 1) % 2]
                        ppt = nps()
                        for h in range(CH):
                            hs = slice(h * 128, (h + 1) * 128)
                            c.mm(ppt[:, hs], R_(Pc[:, hs]), R_(PTc[:, hs]), reads=[Pc, PTc], writes=[ppt])
                        if k < 5:
                            pp = nps()
                            for h in range(CH):
                                hs = slice(h * 128, (h + 1) * 128)
                                c.mm(pp[:, hs], R_(PTc[:, hs]), R_(Pc[:, hs]), reads=[Pc, PTc], writes=[pp])
                        c.copy(R_(PTn[:]), ppt[:, 0:640], reads=[ppt], writes=[PTn], eng="act")
                        if k < 5:
                            c.copy(R_(Pn[:]), pp[:, 0:640], reads=[pp], writes=[Pn])

                    def mu(k):
                        PTk = pairs[k % 2][1]
                        pm = nps()
                        for h in range(CH):
                            hs = slice(h * 128, (h + 1) * 128)
                            c.mm(pm[:, hs], R_(PTk[:, hs]), R_(Mf[:, hs]), reads=[PTk, Mf], writes=[pm])
                        c.tt(R_(Mf[:]), Mf[:], pm[:, 0:640], ALU.add, reads=[Mf, pm], writes=[Mf])
                    sq(0)
                    for k in range(1, 6):
                        sq(k)
                        mu(k)
                    mu(6)
                    c.copy(Mb[:], Mf[:], reads=[Mf], writes=[Mb], eng="act")
                    pks, po = nps(), nps()
                    for h in range(CH):
                        hs = slice(h * 128, (h + 1) * 128)
                        c.mm(pks[:, hs], KTc[:, h, sl], Sb[:, hs], reads=[KTc, Sb], writes=[pks])
                    for h in range(CH):
                        hs = slice(h * 128, (h + 1) * 128)
                        c.stt(X[:, hs], pks[:, hs], sm[:, 5 + h:6 + h], Vtm[:, s, hs], ALU.mult, ALU.add,
                              reads=[pks, sm, Vtm], writes=[X])
                    pvn = nps()
                    for h in range(CH):
                        hs = slice(h * 128, (h + 1) * 128)
                        c.mm(pvn[:, hs], Mb[:, hs], X[:, hs], reads=[Mb, X], writes=[pvn])
                    for h in range(CH):
                        hs = slice(h * 128, (h + 1) * 128)
                        c.ts(vn[:, hs], pvn[:, hs], beta[:, s, h:h + 1], None, ALU.mult, reads=[pvn, beta], writes=[vn])
                    for h in range(CH):
                        hs = slice(h * 128, (h + 1) * 128)
                        c.mm(po[:, hs], Qd[:, hs], Sb[:, hs], start=True, stop=False, reads=[Qd, Sb], writes=[po])
                        c.mm(po[:, hs], attn[:, hs], vn[:, hs], start=False, stop=True, reads=[attn, vn], writes=[po])
                    pds = nps()
                    for h in range(CH):
                        hs = slice(h * 128, (h + 1) * 128)
                        c.mm(pds[:, hs], KD[:, hs], vn[:, hs], reads=[KD, vn], writes=[pds])
                    for h in range(CH):
                        hs = slice(h * 128, (h + 1) * 128)
                        c.stt(Sf[:, hs], Sf[:, hs], sm[:, 15 + h:16 + h], pds[:, hs], ALU.mult, ALU.add,
                              reads=[Sf, sm, pds], writes=[Sf])
                    c.copy(Sb[:], Sf[:], reads=[Sf], writes=[Sb], eng="act")
                    gainbuf[:] = [lv]
                    gatebuf[:] = [zs]
                    out_norm_gate(po, bc_h(lv[:, 256:384], CH), v3(zs[:, s, :], CH), obf, og, t2, ssqC)
                    for h in range(CH):
                        c.tr(ptb[:, h * 128:(h + 1) * 128], obf[:, h * 128:(h + 1) * 128], identb[:], reads=[obf, identb], writes=[ptb])
                    c.copy(OT[:, 11:16, sl], v3(ptb[:, 0:640], CH), reads=[ptb], writes=[OT], eng=evac_eng())


    for ti in range(NTI):
        t0 = ti * NT
        c.dma(xt[:], x_d[t0:t0 + NT, :].rearrange("(s p) d -> p s d", p=128), writes=[xt])
        for l in range(DEPTH):
            lv = lvec[l]
            win = w_in[l]
            with c.phase():
                pst = [c.psum([128, 1024], BF16) for _ in range(2)]
                rmsnorm_to_hT(attn_norm[l], pst)
            if not any(p in phases for p in "ABC"):
                with c.phase():
                    c.emit("dve", lambda e: e.memset(OT[:], 0.0), writes=[OT])
            if "A" in phases:
                phase_A(l, ti)
            elif any(p in phases for p in "BC"):
                c.emit("dve", lambda e: e.memset(OT[:, 0:6, :], 0.0), writes=[OT])
            if "B" in phases:
                phase_B(l, ti)
            elif any(p in phases for p in "AC"):
                c.emit("dve", lambda e: e.memset(OT[:, 6:11, :], 0.0), writes=[OT])
            if "C" in phases:
                phase_C(l, ti)
            elif any(p in phases for p in "AB"):
                c.emit("dve", lambda e: e.memset(OT[:, 11:16, :], 0.0), writes=[OT])
            if "M" in phases:
                with c.phase():
                    mT = c.sbuf([128, KC, NT], BF16, "mT")
                    psr = [c.psum([128, 512], F32) for _ in range(6)]
                    sg = [c.sbuf([128, 4, NT], BF16, f"sg{b}") for b in range(3)]
                    tmpm = [c.sbuf([128, NT], F32) for _ in range(2)]
                    for g in range(4):
                        for br in range(3):
                            def cons(j, ps, br=br):
                                c.act(sg[br][:, j, :], ps[:, 0:NT], AF.Sigmoid, reads=[ps], writes=[sg[br]])
                            o = OFF_GATE + br * D + g * 512
                            proj_fm(win[:, o:o + 512], 512, psr, cons)
                        wb = wload(w_branch[l][:, g * 512:(g + 1) * 512], KC, 512)
                        for j in range(4):
                            first = True
                            for br, (k0, k1) in enumerate(((0, 6), (6, 11), (11, 16))):
                                ps = psr[proj_tm.i % len(psr)]
                                proj_tm.i += 1
                                for kc in range(k0, k1):
                                    c.mm(ps[:, 0:NT], wb[:, kc, j * 128:(j + 1) * 128], OT[:, kc, :],
                                         start=(kc == k0), stop=(kc == k1 - 1), reads=[OT, wb], writes=[ps])
                                acc, tmp = tmpm
                                if br == 0:
                                    c.tt(acc[:], ps[:, 0:NT], sg[br][:, j, :], ALU.mult, reads=[ps, sg[br]], writes=[acc])
                                else:
                                    c.tt(tmp[:], ps[:, 0:NT], sg[br][:, j, :], ALU.mult, reads=[ps, sg[br]], writes=[tmp])
                                    if br == 1:
                                        c.tt(acc[:], acc[:], tmp[:], ALU.add, reads=[acc, tmp], writes=[acc])
                                    else:
                                        c.tt(mT[:, g * 4 + j, :], acc[:], tmp[:], ALU.add, reads=[acc, tmp], writes=[mT])
                    for g in range(4):
                        wb = wload(w_out[l][:, g * 512:(g + 1) * 512], KC, 512)
                        for s in range(4):
                            ps = psr[proj_tm.i % len(psr)]
                            proj_tm.i += 1
                            for kc in range(KC):
                                c.mm(ps[:, 0:512], mT[:, kc, s * 128:(s + 1) * 128], wb[:, kc, :],
                                     start=(kc == 0), stop=(kc == KC - 1), reads=[mT, wb], writes=[ps])
                            c.tt(xt[:, s, g * 512:(g + 1) * 512], xt[:, s, g * 512:(g + 1) * 512], ps[:, 0:512],
                                 ALU.add, reads=[xt, ps], writes=[xt])
            if "F" in phases:
                with c.phase():
                    pst = [c.psum([128, 1024], BF16) for _ in range(2)]
                    rmsnorm_to_hT(ffn_norm[l], pst)
                with c.phase():
                    aT = c.sbuf([128, 44, NT], BF16, "aT")
                    psr = [c.psum([128, 512], F32) for _ in range(8)]
                    sgl = [c.sbuf([128, 4, NT], F32, f"sgl{i}") for i in range(2)]
                    for hg in range(11):
                        sg_ = sgl[hg % 2]

                        def cons_g(j, ps, sg_=sg_):
                            c.act(sg_[:, j, :], ps[:, 0:NT], AF.Silu, reads=[ps], writes=[sg_])

                        def cons_u(j, ps, sg_=sg_, hg=hg):
                            c.tt(aT[:, hg * 4 + j, :], ps[:, 0:NT], sg_[:, j, :], ALU.mult,
                                 reads=[ps, sg_], writes=[aT])
                        proj_fm(w_gate[l][:, hg * 512:(hg + 1) * 512], 512, psr, cons_g)
                        proj_fm(w_up[l][:, hg * 512:(hg + 1) * 512], 512, psr, cons_u)
                    for g in range(4):
                        pss = [psr[(g % 2) * 4 + s] for s in range(4)]
                        for (r0, nk) in ((0, 16), (16, 16), (32, 12)):
                            wb = wload(w_down[l][r0 * 128:(r0 + nk) * 128, g * 512:(g + 1) * 512], nk, 512)
                            for s in range(4):
                                for kc in range(nk):
                                    hc = r0 + kc
                                    c.mm(pss[s][:, 0:512], aT[:, hc, s * 128:(s + 1) * 128], wb[:, kc, :],
                                         start=(hc == 0), stop=(hc == 43), reads=[aT, wb], writes=[pss[s]])
                        for s in range(4):
                            c.tt(xt[:, s, g * 512:(g + 1) * 512], xt[:, s, g * 512:(g + 1) * 512],
                                 pss[s][:, 0:512], ALU.add, reads=[xt, pss[s]], writes=[xt])
        c.dma(y_d[t0:t0 + NT, :].rearrange("(s p) d -> p s d", p=128), xt[:], reads=[xt], writes=[y_buf])
    c.wait_all("sp", [y_buf])
    c.finish()
    return nc


_CACHE = {}


def kernel(**inputs):
    x = np.ascontiguousarray(np.asarray(inputs["x"], dtype=np.float32))
    B, T, _ = x.shape
    DEPTH = int(np.asarray(inputs["w_in"]).shape[0])
    key = (T, DEPTH)
    if key not in _CACHE:
        _CACHE[key] = build(T, DEPTH)
    nc = _CACHE[key]
    tbs = host_tables(T)
    names = ["attn_norm", "w_in", "q_norm", "k_norm", "ret_norm", "conv_w", "a_log", "dt_bias", "gdn_norm",
             "w_branch", "w_out", "ffn_norm", "w_gate", "w_up", "w_down"]
    shared = {n: np.ascontiguousarray(np.asarray(inputs[n], dtype=np.float32)) for n in names}
    for k, v in tbs.items():
        shared["tb_" + k] = v
    n_cores = 8
    in_maps = []
    for cid in range(n_cores):
        m = dict(shared)
        m["x"] = x[cid % B]
        in_maps.append(m)
    res = run_bass_kernel_spmd(nc, in_maps, core_ids=list(range(n_cores)))
    out = np.stack([res.results[b]["y"] for b in range(B)], axis=0)
    return out.astype(np.float32)
```

```python
import numpy as np
from contextlib import ExitStack
import concourse.bass as bass
import concourse.mybir as mybir
from concourse.bass_utils import run_bass_kernel_spmd

F32 = mybir.dt.float32
BF16 = mybir.dt.bfloat16
ALU = mybir.AluOpType
AF = mybir.ActivationFunctionType
AX = mybir.AxisListType


class Buf:
    __slots__ = ("t", "w", "r", "name")

    def __init__(self, t, name=""):
        self.t = t
        self.w = None
        self.r = {}
        self.name = name

    def __getitem__(self, k):
        return self.t[k]


class Eng:
    def __init__(self, name, is_dma=False, is_pe=False):
        self.name = name
        self.is_dma = is_dma
        self.is_pe = is_pe
        self.count = 0
        self.waited = {}
        self.prog = []
        self.sem = None
        self.sems = []
        self.vals = []
        self.n = 0


class Ctx:
    NRING = 8

    def __init__(self, nc):
        self.nc = nc
        self.es = ExitStack()
        self.semh = {}
        self.eng = {}
        for n in ("pe", "act", "dve", "pool"):
            e = Eng(n, is_pe=(n == "pe"))
            e.sem = self._sem("s_" + n)
            self.eng[n] = e
        for n in ("sp", "poolq"):
            e = Eng(n, is_dma=True)
            e.sems = [self._sem(f"d_{n}{i}") for i in range(self.NRING)]
            e.vals = [0] * self.NRING
            self.eng[n] = e
        self.stream = {"pe": "tensor", "act": "scalar", "dve": "vector", "pool": "gpsimd",
                       "sp": "sync", "poolq": "gpsimd"}
        self.order = {"tensor": [], "scalar": [], "vector": [], "gpsimd": [], "sync": []}
        self.nbuf = 0
        self.freed = []
        self.live = [[]]

    def _sem(self, name):
        h = self.es.enter_context(self.nc.semaphore(name))
        k = len(self.semh)
        self.semh[k] = h
        return k

    def sbuf(self, shape, dtype, name=None):
        self.nbuf += 1
        name = f"{name or 'sb'}_{self.nbuf}"
        t = self.es.enter_context(self.nc.sbuf_tensor(name, list(shape), dtype))
        return self._track(Buf(t, name))

    def psum(self, shape, dtype, name=None):
        self.nbuf += 1
        name = f"{name or 'ps'}_{self.nbuf}"
        t = self.es.enter_context(self.nc.psum_tensor(name, list(shape), dtype))
        return self._track(Buf(t, name))

    def _track(self, b):
        ml = self.nc.lookup_mloc(b.t)
        space = str(ml.type)
        lo = int(ml.addr) + (int(ml.bank) * 2048 if "PSUM" in space else 0)
        hi = lo + int(list(ml.dims)[1])
        for (sp, flo, fhi, ev) in self.freed:
            if sp == space and flo < hi and lo < fhi:
                for k, v in ev.items():
                    if b.r.get(k, 0) < v:
                        b.r[k] = v
        self.live[-1].append((b, (space, lo, hi)))
        return b

    def _release(self, items):
        for (b, (space, lo, hi)) in items:
            ev = dict(b.r)
            if b.w is not None and ev.get(b.w[0], 0) < b.w[1]:
                ev[b.w[0]] = b.w[1]
            self.freed = [f for f in self.freed if not (f[0] == space and lo <= f[1] and f[2] <= hi)]
            self.freed.append((space, lo, hi, ev))

    def dram(self, name, shape, dtype, kind="Internal"):
        t = self.nc.dram_tensor(name, list(shape), dtype, kind=kind)
        return Buf(t.ap(), name)

    def emit(self, eng, fn, reads=(), writes=()):
        E = self.eng[eng]
        S = self.order[self.stream[eng]]
        deps = []
        for b in reads:
            if b.w is not None:
                deps.append(b.w)
        for b in writes:
            if b.w is not None:
                deps.append(b.w)
            deps.extend(b.r.items())
        if E.is_dma:
            idx = E.n % self.NRING
            E.n += 1
            sem = E.sems[idx]
            if E.vals[idx] > 0:
                deps.append((sem, E.vals[idx]))
            E.vals[idx] += 16
            val = E.vals[idx]
            inc = 16
        else:
            E.count += 1
            sem = E.sem
            val = E.count
            inc = 1
        W = self._waited(eng)
        for (s, v) in deps:
            if E.is_pe and s == E.sem:
                continue
            if W.get(s, 0) < v:
                S.append(("wait", s, v))
                W[s] = v
        S.append(("op", fn, sem, inc))
        for b in reads:
            if b.r.get(sem, 0) < val:
                b.r[sem] = val
        for b in writes:
            b.w = (sem, val)
            b.r = {}
        return (sem, val)

    def _waited(self, eng):
        st = self.stream[eng]
        if not hasattr(self, "_w"):
            self._w = {k: {} for k in self.order}
        return self._w[st]

    def wait_all(self, eng, bufs):
        S = self.order[self.stream[eng]]
        W = self._waited(eng)
        for b in bufs:
            deps = []
            if b.w is not None:
                deps.append(b.w)
            deps.extend(b.r.items())
            for (s, v) in deps:
                if W.get(s, 0) < v:
                    S.append(("wait", s, v))
                    W[s] = v

    def barrier(self):
        evs = []
        for E in self.eng.values():
            if E.is_dma:
                evs.extend((s, v) for s, v in zip(E.sems, E.vals) if v > 0)
            elif E.count > 0:
                evs.append((E.sem, E.count))
        for st, S in self.order.items():
            W = self._w[st] if hasattr(self, "_w") else self._waited("pe") and self._w[st]
            for (s, v) in evs:
                if W.get(s, 0) < v:
                    S.append(("wait", s, v))
                    W[s] = v

    def phase(self):
        ctx = self

        class _P:
            def __enter__(s):
                s.saved = ctx.es
                ctx.es = ExitStack()
                ctx.live.append([])
                return s

            def __exit__(s, *a):
                ctx._release(ctx.live.pop())
                ctx.es.close()
                ctx.es = s.saved
                return False
        return _P()

    def finish(self):
        nc = self.nc
        semh = self.semh
        order = self.order

        def play(name):
            def run(e):
                for it in order[name]:
                    if it[0] == "wait":
                        e.wait_ge(semh[it[1]], it[2])
                    else:
                        it[1](e).then_inc(semh[it[2]], it[3])
            return run

        with nc.Block() as block:
            block.tensor(play("tensor"))
            block.scalar(play("scalar"))
            block.vector(play("vector"))
            block.gpsimd(play("gpsimd"))
            block.sync(play("sync"))
        self.es.close()

    def mm(self, out, lhsT, rhs, start=True, stop=True, reads=(), writes=()):
        return self.emit("pe", lambda e: e.matmul(out, lhsT=lhsT, rhs=rhs, start=start, stop=stop),
                         reads, writes)

    def tr(self, out, in_, ident, reads=(), writes=()):
        return self.emit("pe", lambda e: e.transpose(out, in_, ident), reads, writes)

    def act(self, out, in_, func, reads=(), writes=(), **kw):
        return self.emit("act", lambda e: e.activation(out=out, in_=in_, func=func, **kw), reads, writes)

    def tt(self, out, in0, in1, op, reads=(), writes=(), eng="dve"):
        return self.emit(eng, lambda e: e.tensor_tensor(out=out, in0=in0, in1=in1, op=op), reads, writes)

    def ts(self, out, in0, s1, s2, op0, op1=None, reads=(), writes=(), eng="dve", **kw):
        if op1 is None:
            return self.emit(eng, lambda e: e.tensor_scalar(out=out, in0=in0, scalar1=s1, scalar2=None,
                                                            op0=op0, **kw), reads, writes)
        return self.emit(eng, lambda e: e.tensor_scalar(out=out, in0=in0, scalar1=s1, scalar2=s2,
                                                        op0=op0, op1=op1, **kw), reads, writes)

    def stt(self, out, in0, scalar, in1, op0, op1, reads=(), writes=(), eng="dve"):
        return self.emit(eng, lambda e: e.scalar_tensor_tensor(out=out, in0=in0, scalar=scalar, in1=in1,
                                                               op0=op0, op1=op1), reads, writes)

    def copy(self, out, in_, reads=(), writes=(), eng="dve"):
        if eng == "act":
            return self.emit("act", lambda e: e.copy(out=out, in_=in_), reads, writes)
        return self.emit(eng, lambda e: e.tensor_copy(out=out, in_=in_), reads, writes)

    def dma(self, out, in_, reads=(), writes=(), q="sp", **kw):
        return self.emit(q, lambda e: e.dma_start(out=out, in_=in_, **kw), reads, writes)


D = 2048
KC = 16
HD = 128
AH, BH, CH = 6, 5, 5
AW, BW, CW = AH * HD, BH * HD, CH * HD
FF = 5632
IN_COLS = 13578
OFF_AQ, OFF_AK, OFF_AV = 0, 768, 1536
OFF_BQ, OFF_BK, OFF_BV, OFF_BG = 2304, 2944, 3584, 4224
OFF_CQKV, OFF_CZ, OFF_CBA, OFF_GATE = 4864, 6784, 7424, 7434
EPS = 1e-6
NT = 512
NEG = -30000.0
RET_LG = [float(np.log1p(-np.exp2(-5.0 - h))) for h in range(BH)]


def host_tables(T):
    f32 = np.float32
    tb = {}
    tb["ident"] = np.eye(128, dtype=f32)
    pos = np.arange(T, dtype=f32)
    rope_freq = (500000.0 ** (-np.arange(0, 32, 2, dtype=f32) / 32)).astype(f32)
    ang = pos[:, None] * rope_freq[None, :]
    tb["ropeA"] = np.concatenate([np.cos(ang), np.sin(ang)], axis=1).astype(f32)
    ret_freq = (10000.0 ** (-np.linspace(0.0, 1.0, 64, dtype=f32))).astype(f32)
    angb = pos[:, None] * ret_freq[None, :]
    tb["ropeB"] = np.concatenate([np.cos(angb), np.sin(angb)], axis=1).astype(f32)
    i = np.arange(128, dtype=np.float64)
    rel = i[None, :] - i[:, None]
    sc = 128.0 ** -0.5
    dec = np.zeros((128, BH, 128), f32)
    xi = np.zeros((128, BH, 128), f32)
    zeta = np.zeros((128, BH), f32)
    for h in range(BH):
        lg = RET_LG[h]
        dec[:, h, :] = np.where(rel >= 0, np.exp(lg * np.maximum(rel, 0)), 0.0) * sc
        xi[:, h, :] = np.exp(lg * (i + 1.0))[None, :]
        zeta[:, h] = np.exp(lg * (127 - i)) * sc
    tb["decT"] = dec.reshape(128, BH * 128)
    tb["xibc"] = xi.reshape(128, BH * 128)
    tb["zeta"] = zeta
    ut = (rel >= 0).astype(f32)
    tb["UT"] = ut
    tb["SUT"] = (rel > 0).astype(f32)
    tb["TRI"] = np.where(rel >= 0, 0.0, NEG).astype(f32)
    nblk = T // 256
    negm = np.zeros((nblk + 1, 16), f32)
    valid = np.zeros((nblk + 1, 16), f32)
    own = np.zeros((nblk + 1, 16), f32)
    for b in range(nblk + 1):
        for n in range(16):
            if n < b:
                valid[b, n] = 1.0
            else:
                negm[b, n] = -1e30
            if n == b:
                own[b, n] = 1.0
    tb["negm"] = negm.reshape(-1)
    tb["valid"] = valid.reshape(-1)
    tb["own"] = own.reshape(-1)
    e = np.zeros((16, 16, 128), f32)
    for n in range(16):
        e[n, n, :] = 1.0
    tb["E"] = e.reshape(16, 16 * 128)
    return tb


def build(T, DEPTH, phases=("A", "B", "C", "M", "F")):
    NTI = T // NT
    NSUBT = T // 128
    NBLK = T // 256
    nc = bass.Bass("TRN2", target_bir_lowering=False)
    c = Ctx(nc)

    def din(name, shape):
        return nc.dram_tensor(name, list(shape), F32, kind="ExternalInput").ap()

    x_d = din("x", [T, D])
    attn_norm = din("attn_norm", [DEPTH, D])
    w_in = din("w_in", [DEPTH, D, IN_COLS])
    q_norm = din("q_norm", [DEPTH, HD])
    k_norm = din("k_norm", [DEPTH, HD])
    ret_norm = din("ret_norm", [DEPTH, BW])
    conv_w = din("conv_w", [DEPTH, 4, 3 * CW])
    a_log = din("a_log", [DEPTH, CH])
    dt_bias = din("dt_bias", [DEPTH, CH])
    gdn_norm = din("gdn_norm", [DEPTH, HD])
    w_branch = din("w_branch", [DEPTH, D, D])
    w_out = din("w_out", [DEPTH, D, D])
    ffn_norm = din("ffn_norm", [DEPTH, D])
    w_gate = din("w_gate", [DEPTH, D, FF])
    w_up = din("w_up", [DEPTH, D, FF])
    w_down = din("w_down", [DEPTH, FF, D])
    tbs = host_tables(T)
    tb_d = {k: din("tb_" + k, v.shape) for k, v in tbs.items()}
    y_d = nc.dram_tensor("y", [T, D], F32, kind="ExternalOutput").ap()
    y_buf = Buf(y_d, "y")

    kth = [c.dram(f"kth{l}", [AH, 128, T], BF16) for l in range(DEPTH)]
    vh = [c.dram(f"vh{l}", [T, AW], BF16) for l in range(DEPTH)]

    xt = c.sbuf([128, 4, D], F32, "xt")
    hT = c.sbuf([128, KC, NT], BF16, "hT")
    OT = c.sbuf([128, KC, NT], BF16, "OT")
    NWB = 2
    wring = [c.sbuf([128, KC, 512], BF16, f"wr{i}") for i in range(NWB)]
    wstate = {"i": 0}
    identf = c.sbuf([128, 128], F32, "identf")
    identb = c.sbuf([128, 128], BF16, "identb")
    onesb = c.sbuf([128, 128], BF16, "onesb")
    onesf = c.sbuf([128, 128], F32, "onesf")
    UTf = c.sbuf([128, 128], F32, "UTf")
    SUTf = c.sbuf([128, 128], F32, "SUTf")
    TRIb = c.sbuf([128, 128], BF16, "TRIb")
    Eb = c.sbuf([16, 16 * 128], BF16, "Eb")
    decT = c.sbuf([128, BW], F32, "decT")
    xibc = c.sbuf([128, BW], F32, "xibc")
    zeta = c.sbuf([128, BH], F32, "zeta")
    negm = c.sbuf([128, (NBLK + 1) * 16], F32, "negm")
    valid = c.sbuf([128, (NBLK + 1) * 16], F32, "valid")
    own = c.sbuf([128, (NBLK + 1) * 16], F32, "own")
    small = c.sbuf([128, 64], F32, "small")
    kmT = [c.sbuf([128, AH, 16], BF16, f"kmT{l}") for l in range(DEPTH)]
    bstate = [c.sbuf([128, BW], F32, f"bst{l}") for l in range(DEPTH)]
    bstate_b = [c.sbuf([128, BW], BF16, f"bstb{l}") for l in range(DEPTH)]
    cstate = [c.sbuf([128, CW], F32, f"cst{l}") for l in range(DEPTH)]
    cstate_b = [c.sbuf([128, CW], BF16, f"cstb{l}") for l in range(DEPTH)]
    carry = [c.sbuf([128, 15, 3], F32, f"carry{l}") for l in range(DEPTH)]
    lvec = [c.sbuf([128, 3 * HD + BW + 2 * CH + 60], F32, f"lvec{l}") for l in range(DEPTH)]

    with c.phase():
        tmpf = c.sbuf([128, 2048], F32, "tmpf")
        c.dma(identf[:], tb_d["ident"], writes=[identf])
        c.copy(identb[:], identf[:], reads=[identf], writes=[identb])
        c.emit("dve", lambda e: e.memset(onesb[:], 1.0), writes=[onesb])
        c.emit("dve", lambda e: e.memset(onesf[:], 1.0), writes=[onesf])
        c.dma(UTf[:], tb_d["UT"], writes=[UTf])
        c.dma(SUTf[:], tb_d["SUT"], writes=[SUTf])
        c.dma(tmpf[:, 0:128], tb_d["TRI"], writes=[tmpf])
        c.copy(TRIb[:], tmpf[:, 0:128], reads=[tmpf], writes=[TRIb])
        c.dma(tmpf[0:16, :], tb_d["E"], writes=[tmpf])
        c.copy(Eb[:], tmpf[0:16, :], reads=[tmpf], writes=[Eb])
        c.dma(decT[:], tb_d["decT"], writes=[decT])
        c.dma(xibc[:], tb_d["xibc"], writes=[xibc])
        c.dma(zeta[:], tb_d["zeta"], writes=[zeta])
        c.dma(negm[:], tb_d["negm"].partition_broadcast(128), writes=[negm])
        c.dma(valid[:], tb_d["valid"].partition_broadcast(128), writes=[valid])
        c.dma(own[:], tb_d["own"].partition_broadcast(128), writes=[own])
        for l in range(DEPTH):
            lv = lvec[l]
            c.dma(lv[:, 0:128], q_norm[l].partition_broadcast(128), writes=[lv])
            c.dma(lv[:, 128:256], k_norm[l].partition_broadcast(128), writes=[lv])
            c.dma(lv[:, 256:384], gdn_norm[l].partition_broadcast(128), writes=[lv])
            c.dma(lv[:, 384:1024], ret_norm[l].partition_broadcast(128), writes=[lv])
            c.dma(lv[:, 1024:1029], a_log[l].partition_broadcast(128), writes=[lv])
            c.dma(lv[:, 1029:1034], dt_bias[l].partition_broadcast(128), writes=[lv])
            for j in range(4):
                c.dma(lv[:, 1034 + j * 15:1034 + (j + 1) * 15], conv_w[l][j].rearrange("(b p) -> p b", p=128),
                      writes=[lv], allow_slow_non_contiguous=True)
            c.act(lv[:, 1024:1029], lv[:, 1024:1029], AF.Exp, reads=[lv], writes=[lv])
            c.ts(lv[:, 1024:1029], lv[:, 1024:1029], -1.0, None, ALU.mult, reads=[lv], writes=[lv])
            c.emit("dve", lambda e, t=kmT[l]: e.memset(t[:], 0.0), writes=[kmT[l]])
            c.emit("dve", lambda e, t=bstate[l]: e.memset(t[:], 0.0), writes=[bstate[l]])
            c.emit("dve", lambda e, t=bstate_b[l]: e.memset(t[:], 0.0), writes=[bstate_b[l]])
            c.emit("dve", lambda e, t=cstate[l]: e.memset(t[:], 0.0), writes=[cstate[l]])
            c.emit("dve", lambda e, t=cstate_b[l]: e.memset(t[:], 0.0), writes=[cstate_b[l]])
            c.emit("dve", lambda e, t=carry[l]: e.memset(t[:], 0.0), writes=[carry[l]])

    def wload(ap, nk, ncols):
        b = wring[wstate["i"] % NWB]
        wstate["i"] += 1
        c.dma(b[:, 0:nk, 0:ncols], ap.rearrange("(k p) n -> p k n", p=128), writes=[b], q="poolq")
        return b

    evac_rr = {"i": 0}

    def evac_eng():
        evac_rr["i"] += 1
        return "act" if evac_rr["i"] % 2 else "dve"

    def rmsnorm_to_hT(gain_ap, pst):
        gain = c.sbuf([128, D], F32, "gain")
        c.dma(gain[:], gain_ap.partition_broadcast(128), writes=[gain])
        ssq = c.sbuf([128, 8], F32)
        junk = c.sbuf([128, D], BF16)
        hb = [c.sbuf([128, D], BF16) for _ in range(2)]
        c.emit("dve", lambda e: e.memset(ssq[:], 0.0), writes=[ssq])
        for s in range(4):
            c.act(junk[:], xt[:, s, :], AF.Square, reads=[xt], writes=[junk, ssq], accum_out=ssq[:, s:s + 1])
        c.act(ssq[:, 4:8], ssq[:, 0:4], AF.Sqrt, reads=[ssq], writes=[ssq], scale=1.0 / D, bias=EPS)
        c.emit("dve", lambda e: e.reciprocal(out=ssq[:, 4:8], in_=ssq[:, 4:8]), reads=[ssq], writes=[ssq])
        for s in range(4):
            h = hb[s % 2]
            c.stt(h[:], xt[:, s, :], ssq[:, 4 + s:5 + s], gain[:], ALU.mult, ALU.mult,
                  reads=[xt, ssq, gain], writes=[h])
            for g in range(2):
                pt = pst[(s * 2 + g) % len(pst)]
                for j in range(8):
                    kc = g * 8 + j
                    c.tr(pt[:, j * 128:(j + 1) * 128], h[:, kc * 128:(kc + 1) * 128], identb[:],
                         reads=[h, identb], writes=[pt])
                c.copy(hT[:, g * 8:(g + 1) * 8, s * 128:(s + 1) * 128],
                       pt[:, 0:1024].rearrange("p (a b) -> p a b", a=8), reads=[pt], writes=[hT],
                       eng=evac_eng())

    def proj_tm(wap, ncols, psr, consume):
        wb = wload(wap, KC, ncols)
        for s in range(4):
            ps = psr[proj_tm.i % len(psr)]
            proj_tm.i += 1
            for kc in range(KC):
                c.mm(ps[:, 0:ncols], hT[:, kc, s * 128:(s + 1) * 128], wb[:, kc, 0:ncols],
                     start=(kc == 0), stop=(kc == KC - 1), reads=[hT, wb], writes=[ps])
            consume(s, ps)
    proj_tm.i = 0

    def proj_fm(wap, ncols, psr, consume, src=None, nk=KC):
        src = src or hT
        wb = wload(wap, nk, ncols)
        for j in range(ncols // 128):
            ps = psr[proj_tm.i % len(psr)]
            proj_tm.i += 1
            for kc in range(nk):
                c.mm(ps[:, 0:NT], wb[:, kc, j * 128:(j + 1) * 128], src[:, kc, :],
                     start=(kc == 0), stop=(kc == nk - 1), reads=[src, wb], writes=[ps])
            consume(j, ps)

    def bc_h(ap2d, nh):
        return ap2d[:, None, :].broadcast_to([128, nh, ap2d.shape[1]])

    def bc_d(ap2d, d=128):
        return ap2d[:, :, None].broadcast_to([128, ap2d.shape[1], d])

    def v3(ap, nh):
        return ap.rearrange("p (h d) -> p h d", h=nh)

    def rstd_from_ssq(dst, ssq_ap, scale, reads, writes, post_mul=None):
        c.act(dst, ssq_ap, AF.Sqrt, reads=reads, writes=writes, scale=scale, bias=EPS)
        c.emit("dve", lambda e: e.reciprocal(out=dst, in_=dst), reads=writes, writes=writes)
        if post_mul is not None:
            c.ts(dst, dst, post_mul, None, ALU.mult, reads=writes, writes=writes)

    def rotary_full(dst_bf, src_f, cs, nh, tmp):
        x1, x2 = src_f[:, :, 0:64], src_f[:, :, 64:128]
        cos = cs[:, None, 0:64].broadcast_to([128, nh, 64])
        sin = cs[:, None, 64:128].broadcast_to([128, nh, 64])
        t1, t2 = tmp
        c.tt(v3(t1[:, 0:nh * 64], nh), x1, cos, ALU.mult, reads=[srcbuf[0], csbuf[0]], writes=[t1])
        c.tt(v3(t2[:, 0:nh * 64], nh), x2, sin, ALU.mult, reads=[srcbuf[0], csbuf[0]], writes=[t2])
        c.tt(dst_bf[:, :, 0:64], v3(t1[:, 0:nh * 64], nh), v3(t2[:, 0:nh * 64], nh), ALU.subtract,
             reads=[t1, t2], writes=[dstbuf[0]])
        c.tt(v3(t1[:, 0:nh * 64], nh), x2, cos, ALU.mult, reads=[srcbuf[0], csbuf[0], dstbuf[0]], writes=[t1])
        c.tt(v3(t2[:, 0:nh * 64], nh), x1, sin, ALU.mult, reads=[srcbuf[0], csbuf[0]], writes=[t2])
        c.tt(dst_bf[:, :, 64:128], v3(t1[:, 0:nh * 64], nh), v3(t2[:, 0:nh * 64], nh), ALU.add,
             reads=[t1, t2], writes=[dstbuf[0]])
    srcbuf, csbuf, dstbuf = [None], [None], [None]

    def out_norm_gate(ops, gain_ap3, gate_ap3, obf, tmpA, tmpB, ssq5):
        og = tmpA
        c.copy(og[:, 0:640], ops[:, 0:640], reads=[ops], writes=[og], eng="act")
        c.tt(tmpB[:, 0:640], og[:, 0:640], og[:, 0:640], ALU.mult, reads=[og], writes=[tmpB])
        c.emit("dve", lambda e: e.reduce_sum(out=ssq5[:, 0:5], in_=v3(tmpB[:, 0:640], 5), axis=AX.X),
               reads=[tmpB], writes=[ssq5])
        rstd_from_ssq(ssq5[:, 0:5], ssq5[:, 0:5], 1.0 / 128, [ssq5], [ssq5])
        c.tt(v3(og[:, 0:640], 5), v3(og[:, 0:640], 5), bc_d(ssq5[:, 0:5]), ALU.mult, reads=[og, ssq5], writes=[og])
        c.tt(v3(og[:, 0:640], 5), v3(og[:, 0:640], 5), gain_ap3, ALU.mult, reads=[og] + gainbuf, writes=[og])
        c.tt(v3(obf[:, 0:640], 5), v3(og[:, 0:640], 5), gate_ap3, ALU.mult, reads=[og] + gatebuf, writes=[obf])
    gainbuf, gatebuf = [], []

    def phase_B(l, ti):
        lv = lvec[l]
        win = w_in[l]
        t0 = ti * NT
        st_f, st_b = bstate[l], bstate_b[l]
        with c.phase():
            qf = c.sbuf([128, 4, BW], BF16, "Bqf")
            kf = c.sbuf([128, 4, BW], BF16, "Bkf")
            vt = c.sbuf([128, 4, BW], BF16, "Bvt")
            gs = c.sbuf([128, 4, BW], BF16, "Bgs")
            cs = c.sbuf([128, 4, 128], F32, "Bcs")
            c.dma(cs[:], tb_d["ropeB"][t0:t0 + NT, :].rearrange("(s p) d -> p s d", p=128), writes=[cs])
            with c.phase():
                psr = [c.psum([128, 512], F32) for _ in range(4)]
                for (dst, off, kind) in ((qf, OFF_BQ, "f"), (kf, OFF_BK, "f"), (vt, OFF_BV, "b"), (gs, OFF_BG, "s")):
                    for (c0, nco) in ((0, 512), (512, 128)):
                        def cons(s, ps, dst=dst, c0=c0, nco=nco, kind=kind):
                            if kind == "s":
                                c.act(dst[:, s, c0:c0 + nco], ps[:, 0:nco], AF.Silu, reads=[ps], writes=[dst])
                            else:
                                c.copy(dst[:, s, c0:c0 + nco], ps[:, 0:nco], reads=[ps], writes=[dst], eng=evac_eng())
                        proj_tm(win[:, off + c0:off + c0 + nco], nco, psr, cons)
            with c.phase():
                qr = c.sbuf([128, 4, BW], BF16, "Bqr")
                kr = c.sbuf([128, 4, BW], BF16, "Bkr")
                kz = c.sbuf([128, 4, BW], BF16, "Bkz")
                QT = c.sbuf([128, BH, NT], BF16, "BQT")
                KT = c.sbuf([128, BH, NT], BF16, "BKT")
                QX = c.sbuf([128, BH, NT], BF16, "BQX")
                t1 = c.sbuf([128, 640], F32, "Bt1")
                t2 = c.sbuf([128, 640], F32, "Bt2")
                scT = c.sbuf([128, BW], BF16, "BscT")
                obf = c.sbuf([128, BW], BF16, "Bobf")
                ssq5 = c.sbuf([128, 8], F32, "Bssq")
                ptb = [c.psum([128, 1024], BF16) for _ in range(2)]
                pA = c.psum([128, 1024], F32, "BpA")
                pB = c.psum([128, 1024], F32, "BpB")
                pC = c.psum([128, 1024], F32, "BpC")
                for s in range(4):
                    for (src, dst) in ((qf, qr), (kf, kr)):
                        srcbuf[0], csbuf[0], dstbuf[0] = src, cs, dst
                        rotary_full(v3(dst[:, s, :], BH), v3(src[:, s, :], BH), cs[:, s, :], BH, (t1, t2))
                    c.tt(v3(kz[:, s, :], BH), v3(kr[:, s, :], BH), bc_d(zeta[:, 0:BH]), ALU.mult,
                         reads=[kr, zeta], writes=[kz])
                    for (src, dst, pt) in ((qr, QT, ptb[0]), (kr, KT, ptb[1])):
                        for h in range(BH):
                            c.tr(pt[:, h * 128:(h + 1) * 128], src[:, s, h * 128:(h + 1) * 128], identb[:],
                                 reads=[src, identb], writes=[pt])
                        c.copy(dst[:, :, s * 128:(s + 1) * 128], v3(pt[:, 0:640], BH), reads=[pt], writes=[dst],
                               eng=evac_eng())
                c.tt(QX[:].rearrange("p h (s t) -> p h s t", s=4), QT[:].rearrange("p h (s t) -> p h s t", s=4),
                     v3(xibc[:], BH)[:, :, None, :].broadcast_to([128, BH, 4, 128]), ALU.mult,
                     reads=[QT, xibc], writes=[QX])
                for s in range(4):
                    sl = slice(s * 128, (s + 1) * 128)
                    for h in range(BH):
                        c.mm(pA[:, h * 128:(h + 1) * 128], KT[:, h, sl], QT[:, h, sl], reads=[KT, QT], writes=[pA])
                    c.tt(scT[:], pA[:, 0:640], decT[:], ALU.mult, reads=[pA, decT], writes=[scT])
                    for h in range(BH):
                        hs = slice(h * 128, (h + 1) * 128)
                        c.mm(pB[:, hs], scT[:, hs], vt[:, s, hs], start=True, stop=False, reads=[scT, vt], writes=[pB])
                        c.mm(pB[:, hs], QX[:, h, sl], st_b[:, hs], start=False, stop=True, reads=[QX, st_b], writes=[pB])
                    for h in range(BH):
                        hs = slice(h * 128, (h + 1) * 128)
                        c.mm(pC[:, hs], kz[:, s, hs], vt[:, s, hs], reads=[kz, vt], writes=[pC])
                    for h in range(BH):
                        hs = slice(h * 128, (h + 1) * 128)
                        c.stt(st_f[:, hs], st_f[:, hs], float(np.exp(RET_LG[h] * 128)), pC[:, hs], ALU.mult, ALU.add,
                              reads=[st_f, pC], writes=[st_f])
                    c.copy(st_b[:], st_f[:], reads=[st_f], writes=[st_b], eng="act")
                    gainbuf[:] = [lv]
                    gatebuf[:] = [gs]
                    out_norm_gate(pB, v3(lv[:, 384:1024], BH), v3(gs[:, s, :], BH), obf, t1, t2, ssq5)
                    pt = ptb[s % 2]
                    for h in range(BH):
                        c.tr(pt[:, h * 128:(h + 1) * 128], obf[:, h * 128:(h + 1) * 128], identb[:],
                             reads=[obf, identb], writes=[pt])
                    c.copy(OT[:, 6:11, sl], v3(pt[:, 0:640], BH), reads=[pt], writes=[OT], eng=evac_eng())

    def phase_A(l, ti):
        lv = lvec[l]
        win = w_in[l]
        t0 = ti * NT
        sc = 128.0 ** -0.5
        with c.phase():
            vt = c.sbuf([128, 4, AW], BF16, "Avt")
            QT = c.sbuf([128, AH, NT], BF16, "AQT")
            KT = c.sbuf([128, AH, NT], BF16, "AKT")
            biasT = c.sbuf([16, AH, NT], BF16, "AbiasT")
            cs = c.sbuf([128, 4, 32], F32, "Acs")
            c.dma(cs[:], tb_d["ropeA"][t0:t0 + NT, :].rearrange("(s p) d -> p s d", p=128), writes=[cs])
            with c.phase():
              qf = c.sbuf([128, 4, AW], F32, "Aqf")
              kf = c.sbuf([128, 4, AW], F32, "Akf")
              if True:
                with c.phase():
                    psr = [c.psum([128, 512], F32) for _ in range(4)]
                    for (dst, off) in ((qf, OFF_AQ), (kf, OFF_AK), (vt, OFF_AV)):
                        for (c0, nco) in ((0, 512), (512, 256)):
                            def cons(s, ps, dst=dst, c0=c0, nco=nco):
                                c.copy(dst[:, s, c0:c0 + nco], ps[:, 0:nco], reads=[ps], writes=[dst], eng=evac_eng())
                            proj_tm(win[:, off + c0:off + c0 + nco], nco, psr, cons)
                c.dma(vh[l][t0:t0 + NT, :].rearrange("(s p) d -> p s d", p=128), vt[:], reads=[vt], writes=[vh[l]])
                with c.phase():
                    tmp = c.sbuf([128, AW], F32, "Atmp")
                    ssq = c.sbuf([128, 8], F32, "Assq")
                    rt = [c.sbuf([128, AH * 16], F32, f"Art{i}") for i in range(4)]
                    qb = c.sbuf([128, AW], BF16, "Aqb")
                    ptb = [c.psum([128, 1024], BF16) for _ in range(2)]
                    pg = c.psum([128, 512], F32, "Apg")
                    gm = c.sbuf([128, AH * 16], F32, "Agm")
                    m8 = c.sbuf([128, AH * 8], F32, "Am8")
                    bb = c.sbuf([128, AH * 16], BF16, "Abb")
                    kmf = c.sbuf([128, AH * 2], F32, "Akmf")
                    for s in range(4):
                        sl = slice(s * 128, (s + 1) * 128)
                        for (src, dst, g0, post) in ((kf, KT, 128, None), (qf, QT, 0, sc)):
                            x3 = v3(src[:, s, :], AH)
                            c.tt(tmp[:], src[:, s, :], src[:, s, :], ALU.mult, reads=[src], writes=[tmp])
                            c.emit("dve", lambda e: e.reduce_sum(out=ssq[:, 0:AH], in_=v3(tmp[:], AH), axis=AX.X),
                                   reads=[tmp], writes=[ssq])
                            rstd_from_ssq(ssq[:, 0:AH], ssq[:, 0:AH], 1.0 / 128, [ssq], [ssq], post_mul=post)
                            c.tt(x3, x3, bc_d(ssq[:, 0:AH]), ALU.mult, reads=[src, ssq], writes=[src])
                            c.tt(x3, x3, bc_h(lv[:, g0:g0 + 128], AH), ALU.mult, reads=[src, lv], writes=[src])
                            x1, x2 = x3[:, :, 0:16], x3[:, :, 16:32]
                            cos = cs[:, s, None, 0:16].broadcast_to([128, AH, 16])
                            sin = cs[:, s, None, 16:32].broadcast_to([128, AH, 16])
                            r = [v3(t[:], AH) for t in rt]
                            c.tt(r[0], x1, cos, ALU.mult, reads=[src, cs], writes=[rt[0]])
                            c.tt(r[1], x2, sin, ALU.mult, reads=[src, cs], writes=[rt[1]])
                            c.tt(r[2], x2, cos, ALU.mult, reads=[src, cs], writes=[rt[2]])
                            c.tt(r[3], x1, sin, ALU.mult, reads=[src, cs], writes=[rt[3]])
                            q3 = v3(qb[:], AH)
                            c.tt(q3[:, :, 0:16], r[0], r[1], ALU.subtract, reads=[rt[0], rt[1]], writes=[qb])
                            c.tt(q3[:, :, 16:32], r[2], r[3], ALU.add, reads=[rt[2], rt[3]], writes=[qb])
                            c.copy(q3[:, :, 32:128], x3[:, :, 32:128], reads=[src], writes=[qb], eng="act")
                            pt = ptb[0] if dst is KT else ptb[1]
                            for h in range(AH):
                                c.tr(pt[:, h * 128:(h + 1) * 128], qb[:, h * 128:(h + 1) * 128], identb[:],
                                     reads=[qb, identb], writes=[pt])
                            c.copy(dst[:, :, sl], v3(pt[:, 0:768], AH), reads=[pt], writes=[dst], eng=evac_eng())
                    c.emit("dve", lambda e: e.reduce_sum(out=kmf[:].rearrange("p (h b) -> p h b", h=AH),
                                                         in_=KT[:].rearrange("p h (b t) -> p h b t", b=2), axis=AX.X),
                           reads=[KT], writes=[kmf])
                    c.ts(kmT[l][:, :, 2 * ti:2 * ti + 2], kmf[:].rearrange("p (h b) -> p h b", h=AH), 1.0 / 256, None,
                         ALU.mult, reads=[kmf], writes=[kmT[l]])
                    c.dma(kth[l][:, :, t0:t0 + NT].rearrange("h p t -> p h t"), KT[:], reads=[KT], writes=[kth[l]])
                    for s in range(4):
                        sl = slice(s * 128, (s + 1) * 128)
                        blk = 2 * ti + s // 2
                        bsl = slice(blk * 16, (blk + 1) * 16)
                        for h in range(AH):
                            c.mm(pg[:, h * 16:(h + 1) * 16], QT[:, h, sl], kmT[l][:, h, :], reads=[QT, kmT[l]], writes=[pg])
                        g3 = v3(gm[:], AH)
                        c.tt(g3, v3(pg[:, 0:AH * 16], AH), bc_h(negm[:, bsl], AH), ALU.add, reads=[pg, negm], writes=[gm])
                        for h in range(AH):
                            c.emit("dve", lambda e, h=h: e.max(out=m8[:, h * 8:(h + 1) * 8], in_=gm[:, h * 16:(h + 1) * 16]),
                                   reads=[gm], writes=[m8])
                        c.tt(g3, g3, v3(m8[:], AH)[:, :, 2:3].broadcast_to([128, AH, 16]), ALU.is_ge,
                             reads=[gm, m8], writes=[gm])
                        c.tt(g3, g3, bc_h(valid[:, bsl], AH), ALU.mult, reads=[gm, valid], writes=[gm])
                        c.tt(g3, g3, bc_h(own[:, bsl], AH), ALU.add, reads=[gm, own], writes=[gm])
                        c.ts(bb[:], gm[:], -NEG, NEG, ALU.mult, ALU.add, reads=[gm], writes=[bb])
                        pt = ptb[s % 2]
                        for h in range(AH):
                            c.tr(pt[0:16, h * 128:(h + 1) * 128], bb[:, h * 16:(h + 1) * 16], identb[:],
                                 reads=[bb, identb], writes=[pt])
                        c.copy(biasT[:, :, sl], v3(pt[0:16, 0:768], AH), reads=[pt], writes=[biasT], eng=evac_eng())
            with c.phase():
                npast = 4 * ti
                KH = [c.sbuf([128, max(npast, 1) * 128], BF16, f"AKH{i}") for i in range(2)]
                VH = [c.sbuf([128, max(npast, 1), 128], BF16, f"AVH{i}") for i in range(2)]
                PT = [c.sbuf([128, NT], BF16, f"APT{i}") for i in range(3)]
                rden = c.sbuf([128, NT], F32, "Arden")
                pST = [c.psum([128, 512], F32, f"ApST{i}") for i in range(3)]
                pO = [c.psum([128, 512], F32, f"ApO{i}") for i in range(2)]
                pD = [c.psum([128, 512], F32, f"ApD{i}") for i in range(2)]
                itc = {"it": 0}
                for h in range(AH):
                    kh, vhh = KH[h % 2], VH[h % 2]
                    if npast:
                        c.dma(kh[:, 0:npast * 128], kth[l][h, :, 0:npast * 128], reads=[kth[l]], writes=[kh])
                        c.dma(vhh[:, 0:npast, :],
                              vh[l][0:npast * 128, h * 128:(h + 1) * 128].rearrange("(k p) d -> p k d", p=128),
                              reads=[vh[l]], writes=[vhh])
                    po, pd = pO[h % 2], pD[h % 2]
                    nkt = npast + 4
                    def score_step(kt):
                            j = kt - npast
                            n = kt // 2
                            st = pST[itc["it"] % 3]
                            pt_ = PT[itc["it"] % 3]
                            itc["it"] += 1
                            if j < 0:
                                klhs, vk, q0 = kh[:, kt * 128:(kt + 1) * 128], vhh[:, kt, :], 0
                                kr_, vr_ = [kh], [vhh]
                                c.mm(st[:, 0:NT], klhs, QT[:, h, :], start=True, stop=False, reads=kr_ + [QT], writes=[st])
                                c.mm(st[:, 0:NT], Eb[:, n * 128:(n + 1) * 128], biasT[:, h, :], start=False, stop=True,
                                     reads=[Eb, biasT], writes=[st])
                            else:
                                klhs, vk, q0 = KT[:, h, j * 128:(j + 1) * 128], vt[:, j, h * 128:(h + 1) * 128], j * 128
                                kr_, vr_ = [KT], [vt]
                                usebias = j < 2
                                c.mm(st[:, q0:q0 + 128], klhs, QT[:, h, q0:q0 + 128], start=True, stop=False,
                                     reads=kr_ + [QT], writes=[st])
                                if usebias:
                                    c.mm(st[:, q0:q0 + 128], Eb[:, n * 128:(n + 1) * 128], biasT[:, h, q0:q0 + 128],
                                         start=False, stop=False, reads=[Eb, biasT], writes=[st])
                                c.mm(st[:, q0:q0 + 128], identb[:], TRIb[:], start=False, stop=True,
                                     reads=[identb, TRIb], writes=[st])
                                if q0 + 128 < NT:
                                    c.mm(st[:, q0 + 128:NT], klhs, QT[:, h, q0 + 128:NT], start=True, stop=not usebias,
                                         reads=kr_ + [QT], writes=[st])
                                    if usebias:
                                        c.mm(st[:, q0 + 128:NT], Eb[:, n * 128:(n + 1) * 128], biasT[:, h, q0 + 128:NT],
                                             start=False, stop=True, reads=[Eb, biasT], writes=[st])
                            c.act(pt_[:, q0:NT], st[:, q0:NT], AF.Exp, reads=[st], writes=[pt_])
                            return (vk, vr_, pt_, q0)

                    def pv_step(kt, vk, vr_, pt_, q0):
                            c.mm(po[:, q0:NT], vk, pt_[:, q0:NT], start=(kt == 0), stop=(kt == nkt - 1),
                                 reads=vr_ + [pt_], writes=[po])
                            c.mm(pd[:, q0:NT], onesb[:], pt_[:, q0:NT], start=(kt == 0), stop=(kt == nkt - 1),
                                 reads=[onesb, pt_], writes=[pd])

                    prev = None
                    for kt in range(nkt):
                        cur = score_step(kt)
                        if prev is not None:
                            pv_step(kt - 1, *prev)
                        prev = cur
                    pv_step(nkt - 1, *prev)
                    c.emit("dve", lambda e, pd=pd: e.reciprocal(out=rden[:], in_=pd[:, 0:NT]), reads=[pd], writes=[rden])
                    c.tt(OT[:, h, :], po[:, 0:NT], rden[:], ALU.mult, reads=[po, rden], writes=[OT])

    def phase_C(l, ti):
        lv = lvec[l]
        win = w_in[l]
        t0 = ti * NT
        sc = 128.0 ** -0.5
        Sf, Sb = cstate[l], cstate_b[l]
        cwo = 1034
        with c.phase():
            qkvT = c.sbuf([128, 15, NT], BF16, "CqkvT")
            zs = c.sbuf([128, 4, CW], BF16, "Czs")
            ba = c.sbuf([128, 4, 16], F32, "Cba")
            beta = c.sbuf([128, 4, CH], F32, "Cbeta")
            nbeta = c.sbuf([128, 4, CH], F32, "Cnbeta")
            gg = c.sbuf([128, 4, CH], F32, "Cgg")
            QTc = c.sbuf([128, CH, NT], BF16, "CQT")
            KTc = c.sbuf([128, CH, NT], BF16, "CKT")
            Ktm = c.sbuf([128, 4, CW], BF16, "CKtm")
            Vtm = c.sbuf([128, 4, CW], BF16, "CVtm")
            with c.phase():
                psr = [c.psum([128, 512], F32) for _ in range(4)]
                xcb = [c.sbuf([128, NT + 3], F32, f"Cxcb{i}") for i in range(2)]
                acc = [c.sbuf([128, NT], F32, f"Cacc{i}") for i in range(2)]
                for (c0, nco) in ((0, 512), (512, 512), (1024, 512), (1536, 384)):
                    def cons(j, ps, c0=c0):
                        b = c0 // 128 + j
                        xb, ac = xcb[b % 2], acc[b % 2]
                        c.copy(xb[:, 3:NT + 3], ps[:, 0:NT], reads=[ps], writes=[xb], eng="act")
                        c.copy(xb[:, 0:3], carry[l][:, b, :], reads=[carry[l]], writes=[xb])
                        c.copy(carry[l][:, b, :], xb[:, NT:NT + 3], reads=[xb], writes=[carry[l]])
                        c.ts(ac[:], xb[:, 3:NT + 3], lv[:, cwo + 3 * 15 + b:cwo + 3 * 15 + b + 1], None, ALU.mult,
                             reads=[xb, lv], writes=[ac])
                        for jj in (2, 1, 0):
                            c.stt(ac[:], xb[:, jj:jj + NT], lv[:, cwo + jj * 15 + b:cwo + jj * 15 + b + 1], ac[:],
                                  ALU.mult, ALU.add, reads=[xb, lv, ac], writes=[ac])
                        c.act(qkvT[:, b, :], ac[:], AF.Silu, reads=[ac], writes=[qkvT])
                    proj_fm(win[:, OFF_CQKV + c0:OFF_CQKV + c0 + nco], nco, psr, cons)
                for (c0, nco) in ((0, 512), (512, 128)):
                    def consz(s, ps, c0=c0, nco=nco):
                        c.act(zs[:, s, c0:c0 + nco], ps[:, 0:nco], AF.Silu, reads=[ps], writes=[zs])
                    proj_tm(win[:, OFF_CZ + c0:OFF_CZ + c0 + nco], nco, psr, consz)

                def consb(s, ps):
                    c.copy(ba[:, s, 0:10], ps[:, 0:10], reads=[ps], writes=[ba])
                proj_tm(win[:, OFF_CBA:OFF_CBA + 10], 10, psr, consb)
                c.act(beta[:], ba[:, :, 0:5], AF.Sigmoid, reads=[ba], writes=[beta])
                c.ts(nbeta[:], beta[:], -1.0, None, ALU.mult, reads=[beta], writes=[nbeta])
                c.tt(gg[:], ba[:, :, 5:10], lv[:, None, 1029:1034].broadcast_to([128, 4, CH]), ALU.add,
                     reads=[ba, lv], writes=[gg])
                c.act(gg[:], gg[:], AF.Exp, reads=[gg], writes=[gg])
                c.act(gg[:], gg[:], AF.Ln, reads=[gg], writes=[gg], bias=1.0)
                c.tt(gg[:], gg[:], lv[:, None, 1024:1029].broadcast_to([128, 4, CH]), ALU.mult,
                     reads=[gg, lv], writes=[gg])
                sq = [c.sbuf([128, NT], BF16, f"Csq{i}") for i in range(2)]
                rs = [c.sbuf([128, NT], F32, f"Crs{i}") for i in range(2)]
                for b in range(10):
                    sq_, rs_ = sq[b % 2], rs[b % 2]
                    ps = psr[b % 4]
                    c.tt(sq_[:], qkvT[:, b, :], qkvT[:, b, :], ALU.mult, reads=[qkvT], writes=[sq_])
                    c.mm(ps[:, 0:NT], onesb[:], sq_[:], reads=[onesb, sq_], writes=[ps])
                    c.act(rs_[:], ps[:, 0:NT], AF.Sqrt, reads=[ps], writes=[rs_], scale=1.0, bias=EPS)
                    c.emit("dve", lambda e, rs_=rs_: e.reciprocal(out=rs_[:], in_=rs_[:]), reads=[rs_], writes=[rs_])
                    if b < 5:
                        c.stt(QTc[:, b, :], qkvT[:, b, :], sc, rs_[:], ALU.mult, ALU.mult, reads=[qkvT, rs_], writes=[QTc])
                    else:
                        c.tt(KTc[:, b - 5, :], qkvT[:, b, :], rs_[:], ALU.mult, reads=[qkvT, rs_], writes=[KTc])
            with c.phase():
                ptb = [c.psum([128, 1024], BF16) for _ in range(2)]
                for s in range(4):
                    sl = slice(s * 128, (s + 1) * 128)
                    for (src, b0, dst, pt) in ((KTc, 0, Ktm, ptb[0]), (qkvT, 10, Vtm, ptb[1])):
                        for h in range(CH):
                            c.tr(pt[:, h * 128:(h + 1) * 128], src[:, b0 + h, sl], identb[:], reads=[src, identb], writes=[pt])
                        c.copy(dst[:, s, :], pt[:, 0:640], reads=[pt], writes=[dst], eng=evac_eng())
            with c.phase():
                P2 = [c.psum([128, 1024], F32, f"CP{i}") for i in range(3)]
                ptb = c.psum([128, 1024], BF16, "Cptb")
                psm = c.psum([128, 512], F32, "Cpsm")
                pi = {"i": 0}

                def nps():
                    pi["i"] += 1
                    return P2[pi["i"] % 3]
                f = lambda n: c.sbuf([128, CW], F32, n)
                b16 = lambda n: c.sbuf([128, CW], BF16, n)
                GU, Dm, LT, LTs, Mf = f("CGU"), f("CDm"), f("CLT"), f("CLTs"), c.sbuf([128, CW], mybir.dt.float32r, "CMf")
                egb = GU
                og, t2 = LT, LTs
                fr = lambda n: c.sbuf([128, CW], mybir.dt.float32r, n)
                Nb, NTb, Pb, PTb = fr("CNb"), fr("CNTb"), fr("CPb"), fr("CPTb")
                Mb, attn, Qd, X, vn, KD, obf = (b16("CMb"), b16("Cattn"), b16("CQd"), b16("CX"), b16("Cvn"), b16("CKD"), b16("Cobf"))
                ssqC = c.sbuf([128, 8], F32, "CssqC")
                sm = c.sbuf([128, 32], F32, "Csm")
                for s in range(4):
                    sl = slice(s * 128, (s + 1) * 128)
                    g_s = gg[:, s, :]
                    c.tt(v3(GU[:], CH), bc_h(UTf[:], CH), bc_d(g_s), ALU.mult, reads=[UTf, gg], writes=[GU])
                    pg = nps()
                    for h in range(CH):
                        c.mm(pg[:, h * 128:(h + 1) * 128], onesf[:], GU[:, h * 128:(h + 1) * 128], reads=[onesf, GU], writes=[pg])
                    c.mm(psm[:, 0:5], UTf[:], g_s, reads=[UTf, gg], writes=[psm])
                    c.copy(sm[:, 0:5], psm[:, 0:5], reads=[psm], writes=[sm])
                    for h in range(CH):
                        hs = slice(h * 128, (h + 1) * 128)
                        c.ts(Dm[:, hs], pg[:, hs], sm[:, h:h + 1], 0.0, ALU.subtract, ALU.min, reads=[pg, sm], writes=[Dm])
                    c.act(LT[:], Dm[:], AF.Exp, reads=[Dm], writes=[LT])
                    c.tt(v3(LTs[:], CH), v3(LT[:], CH), bc_h(SUTf[:], CH), ALU.mult, reads=[LT, SUTf], writes=[LTs])
                    c.tt(v3(LT[:], CH), v3(LT[:], CH), bc_h(UTf[:], CH), ALU.mult, reads=[LT, UTf], writes=[LT])
                    c.act(egb[:], pg[:, 0:640], AF.Exp, reads=[pg], writes=[egb])
                    c.act(sm[:, 5:10], sm[:, 0:5], AF.Exp, reads=[sm], writes=[sm])
                    c.ts(sm[:, 5:10], sm[:, 5:10], -1.0, None, ALU.mult, reads=[sm], writes=[sm])
                    glast = v3(pg[:, 0:640], CH)[:, :, 127]
                    c.tt(sm[:, 10:15], glast, sm[:, 0:5], ALU.subtract, reads=[pg, sm], writes=[sm])
                    c.act(sm[:, 10:15], sm[:, 10:15], AF.Exp, reads=[sm], writes=[sm])
                    c.act(sm[:, 15:20], glast, AF.Exp, reads=[pg], writes=[sm])
                    pkk, pqk = nps(), nps()
                    for h in range(CH):
                        hs = slice(h * 128, (h + 1) * 128)
                        c.mm(pkk[:, hs], KTc[:, h, sl], KTc[:, h, sl], reads=[KTc], writes=[pkk])
                        c.mm(pqk[:, hs], KTc[:, h, sl], QTc[:, h, sl], reads=[KTc, QTc], writes=[pqk])
                    for h in range(CH):
                        hs = slice(h * 128, (h + 1) * 128)
                        c.stt(Nb[:, hs], pkk[:, hs], nbeta[:, s, h:h + 1], LTs[:, hs], ALU.mult, ALU.mult,
                              reads=[pkk, nbeta, LTs], writes=[Nb])
                    c.tt(attn[:], pqk[:, 0:640], LT[:], ALU.mult, reads=[pqk, LT], writes=[attn])
                    c.tt(v3(Qd[:], CH), QTc[:, :, sl], v3(egb[:], CH), ALU.mult, reads=[QTc, egb], writes=[Qd])
                    c.tt(v3(KD[:], CH), v3(Ktm[:, s, :], CH), bc_d(sm[:, 10:15]), ALU.mult, reads=[Ktm, sm], writes=[KD])
                    R_ = lambda ap: ap
                    pnt = nps()
                    for h in range(CH):
                        c.tr(pnt[:, h * 128:(h + 1) * 128], Nb[:, h * 128:(h + 1) * 128].bitcast(F32), identf[:], reads=[Nb, identf], writes=[pnt])
                    c.copy(R_(NTb[:]), pnt[:, 0:640], reads=[pnt], writes=[NTb], eng="act")
                    c.tt(R_(v3(Mf[:], CH)), v3(Nb[:], CH), bc_h(identf[:], CH), ALU.add, reads=[Nb, identf], writes=[Mf])
                    pairs = [(Nb, NTb), (Pb, PTb)]

                    def sq(k):
                        Pc, PTc = pairs[k % 2]
                        Pn, PTn = pairs[(k + 1) % 2]
                        ppt = nps()
                        for h in range(CH):
                            hs = slice(h * 128, (h + 1) * 128)
                            c.mm(ppt[:, hs], R_(Pc[:, hs]), R_(PTc[:, hs]), reads=[Pc, PTc], writes=[ppt])
                        if k < 5:
                            pp = nps()
                            for h in range(CH):
                                hs = slice(h * 128, (h + 1) * 128)
                                c.mm(pp[:, hs], R_(PTc[:, hs]), R_(Pc[:, hs]), reads=[Pc, PTc], writes=[pp])
                        c.copy(R_(PTn[:]), ppt[:, 0:640], reads=[ppt], writes=[PTn], eng="act")
                        if k < 5:
                            c.copy(R_(Pn[:]), pp[:, 0:640], reads=[pp], writes=[Pn])

                    def mu(k):
                        PTk = pairs[k % 2][1]
                        pm = nps()
                        for h in range(CH):
                            hs = slice(h * 128, (h + 1) * 128)
                            c.mm(pm[:, hs], R_(PTk[:, hs]), R_(Mf[:, hs]), reads=[PTk, Mf], writes=[pm])
                        c.tt(R_(Mf[:]), Mf[:], pm[:, 0:640], ALU.add, reads=[Mf, pm], writes=[Mf])
                    sq(0)
                    for k in range(1, 6):
                        sq(k)
                        mu(k)
                    mu(6)
                    c.copy(Mb[:], Mf[:], reads=[Mf], writes=[Mb], eng="act")
                    pks, po = nps(), nps()
                    for h in range(CH):
                        hs = slice(h * 128, (h + 1) * 128)
                        c.mm(pks[:, hs], KTc[:, h, sl], Sb[:, hs], reads=[KTc, Sb], writes=[pks])
                    for h in range(CH):
                        hs = slice(h * 128, (h + 1) * 128)
                        c.stt(X[:, hs], pks[:, hs], sm[:, 5 + h:6 + h], Vtm[:, s, hs], ALU.mult, ALU.add,
                              reads=[pks, sm, Vtm], writes=[X])
                    pvn = nps()
                    for h in range(CH):
                        hs = slice(h * 128, (h + 1) * 128)
                        c.mm(pvn[:, hs], Mb[:, hs], X[:, hs], reads=[Mb, X], writes=[pvn])
                    for h in range(CH):
                        hs = slice(h * 128, (h + 1) * 128)
                        c.ts(vn[:, hs], pvn[:, hs], beta[:, s, h:h + 1], None, ALU.mult, reads=[pvn, beta], writes=[vn])
                    for h in range(CH):
                        hs = slice(h * 128, (h + 1) * 128)
                        c.mm(po[:, hs], Qd[:, hs], Sb[:, hs], start=True, stop=False, reads=[Qd, Sb], writes=[po])
                        c.mm(po[:, hs], attn[:, hs], vn[:, hs], start=False, stop=True, reads=[attn, vn], writes=[po])
                    pds = nps()
                    for h in range(CH):
                        hs = slice(h * 128, (h + 1) * 128)
                        c.mm(pds[:, hs], KD[:, hs], vn[:, hs], reads=[KD, vn], writes=[pds])
                    for h in range(CH):
                        hs = slice(h * 128, (h + 1) * 128)
                        c.stt(Sf[:, hs], Sf[:, hs], sm[:, 15 + h:16 + h], pds[:, hs], ALU.mult, ALU.add,
                              reads=[Sf, sm, pds], writes=[Sf])
                    c.copy(Sb[:], Sf[:], reads=[Sf], writes=[Sb], eng="act")
                    gainbuf[:] = [lv]
                    gatebuf[:] = [zs]
                    out_norm_gate(po, bc_h(lv[:, 256:384], CH), v3(zs[:, s, :], CH), obf, og, t2, ssqC)
                    for h in range(CH):
                        c.tr(ptb[:, h * 128:(h + 1) * 128], obf[:, h * 128:(h + 1) * 128], identb[:], reads=[obf, identb], writes=[ptb])
                    c.copy(OT[:, 11:16, sl], v3(ptb[:, 0:640], CH), reads=[ptb], writes=[OT], eng=evac_eng())


    for ti in range(NTI):
        t0 = ti * NT
        c.dma(xt[:], x_d[t0:t0 + NT, :].rearrange("(s p) d -> p s d", p=128), writes=[xt])
        for l in range(DEPTH):
            lv = lvec[l]
            win = w_in[l]
            with c.phase():
                pst = [c.psum([128, 1024], BF16) for _ in range(2)]
                rmsnorm_to_hT(attn_norm[l], pst)
            if not any(p in phases for p in "ABC"):
                with c.phase():
                    c.emit("dve", lambda e: e.memset(OT[:], 0.0), writes=[OT])
            if "A" in phases:
                phase_A(l, ti)
            elif any(p in phases for p in "BC"):
                c.emit("dve", lambda e: e.memset(OT[:, 0:6, :], 0.0), writes=[OT])
            if "B" in phases:
                phase_B(l, ti)
            elif any(p in phases for p in "AC"):
                c.emit("dve", lambda e: e.memset(OT[:, 6:11, :], 0.0), writes=[OT])
            if "C" in phases:
                phase_C(l, ti)
            elif any(p in phases for p in "AB"):
                c.emit("dve", lambda e: e.memset(OT[:, 11:16, :], 0.0), writes=[OT])
            if "M" in phases:
                with c.phase():
                    mT = c.sbuf([128, KC, NT], BF16, "mT")
                    psr = [c.psum([128, 512], F32) for _ in range(6)]
                    sg = [c.sbuf([128, 4, NT], BF16, f"sg{b}") for b in range(3)]
                    tmpm = [c.sbuf([128, NT], F32) for _ in range(2)]
                    for g in range(4):
                        for br in range(3):
                            def cons(j, ps, br=br):
                                c.act(sg[br][:, j, :], ps[:, 0:NT], AF.Sigmoid, reads=[ps], writes=[sg[br]])
                            o = OFF_GATE + br * D + g * 512
                            proj_fm(win[:, o:o + 512], 512, psr, cons)
                        wb = wload(w_branch[l][:, g * 512:(g + 1) * 512], KC, 512)
                        for j in range(4):
                            first = True
                            for br, (k0, k1) in enumerate(((0, 6), (6, 11), (11, 16))):
                                ps = psr[proj_tm.i % len(psr)]
                                proj_tm.i += 1
                                for kc in range(k0, k1):
                                    c.mm(ps[:, 0:NT], wb[:, kc, j * 128:(j + 1) * 128], OT[:, kc, :],
                                         start=(kc == k0), stop=(kc == k1 - 1), reads=[OT, wb], writes=[ps])
                                acc, tmp = tmpm
                                if br == 0:
                                    c.tt(acc[:], ps[:, 0:NT], sg[br][:, j, :], ALU.mult, reads=[ps, sg[br]], writes=[acc])
                                else:
                                    c.tt(tmp[:], ps[:, 0:NT], sg[br][:, j, :], ALU.mult, reads=[ps, sg[br]], writes=[tmp])
                                    if br == 1:
                                        c.tt(acc[:], acc[:], tmp[:], ALU.add, reads=[acc, tmp], writes=[acc])
                                    else:
                                        c.tt(mT[:, g * 4 + j, :], acc[:], tmp[:], ALU.add, reads=[acc, tmp], writes=[mT])
                    for g in range(4):
                        wb = wload(w_out[l][:, g * 512:(g + 1) * 512], KC, 512)
                        for s in range(4):
                            ps = psr[proj_tm.i % len(psr)]
                            proj_tm.i += 1
                            for kc in range(KC):
                                c.mm(ps[:, 0:512], mT[:, kc, s * 128:(s + 1) * 128], wb[:, kc, :],
                                     start=(kc == 0), stop=(kc == KC - 1), reads=[mT, wb], writes=[ps])
                            c.tt(xt[:, s, g * 512:(g + 1) * 512], xt[:, s, g * 512:(g + 1) * 512], ps[:, 0:512],
                                 ALU.add, reads=[xt, ps], writes=[xt])
            if "F" in phases:
                with c.phase():
                    pst = [c.psum([128, 1024], BF16) for _ in range(2)]
                    rmsnorm_to_hT(ffn_norm[l], pst)
                with c.phase():
                    aT = c.sbuf([128, 44, NT], BF16, "aT")
                    psr = [c.psum([128, 512], F32) for _ in range(8)]
                    sgl = [c.sbuf([128, 4, NT], F32, f"sgl{i}") for i in range(2)]
                    for hg in range(11):
                        sg_ = sgl[hg % 2]

                        def cons_g(j, ps, sg_=sg_):
                            c.act(sg_[:, j, :], ps[:, 0:NT], AF.Silu, reads=[ps], writes=[sg_])

                        def cons_u(j, ps, sg_=sg_, hg=hg):
                            c.tt(aT[:, hg * 4 + j, :], ps[:, 0:NT], sg_[:, j, :], ALU.mult,
                                 reads=[ps, sg_], writes=[aT])
                        proj_fm(w_gate[l][:, hg * 512:(hg + 1) * 512], 512, psr, cons_g)
                        proj_fm(w_up[l][:, hg * 512:(hg + 1) * 512], 512, psr, cons_u)
                    for g in range(4):
                        pss = [psr[(g % 2) * 4 + s] for s in range(4)]
                        for (r0, nk) in ((0, 16), (16, 16), (32, 12)):
                            wb = wload(w_down[l][r0 * 128:(r0 + nk) * 128, g * 512:(g + 1) * 512], nk, 512)
                            for s in range(4):
                                for kc in range(nk):
                                    hc = r0 + kc
                                    c.mm(pss[s][:, 0:512], aT[:, hc, s * 128:(s + 1) * 128], wb[:, kc, :],
                                         start=(hc == 0), stop=(hc == 43), reads=[aT, wb], writes=[pss[s]])
                        for s in range(4):
                            c.tt(xt[:, s, g * 512:(g + 1) * 512], xt[:, s, g * 512:(g + 1) * 512],
                                 pss[s][:, 0:512], ALU.add, reads=[xt, pss[s]], writes=[xt])
        c.dma(y_d[t0:t0 + NT, :].rearrange("(s p) d -> p s d", p=128), xt[:], reads=[xt], writes=[y_buf])
    c.wait_all("sp", [y_buf])
    c.finish()
    return nc


_CACHE = {}


def kernel(**inputs):
    x = np.ascontiguousarray(np.asarray(inputs["x"], dtype=np.float32))
    B, T, _ = x.shape
    DEPTH = int(np.asarray(inputs["w_in"]).shape[0])
    key = (T, DEPTH)
    if key not in _CACHE:
        _CACHE[key] = build(T, DEPTH)
    nc = _CACHE[key]
    tbs = host_tables(T)
    names = ["attn_norm", "w_in", "q_norm", "k_norm", "ret_norm", "conv_w", "a_log", "dt_bias", "gdn_norm",
             "w_branch", "w_out", "ffn_norm", "w_gate", "w_up", "w_down"]
    shared = {n: np.ascontiguousarray(np.asarray(inputs[n], dtype=np.float32)) for n in names}
    for k, v in tbs.items():
        shared["tb_" + k] = v
    n_cores = 8
    place = {0: 0, 2: 1, 4: 2, 6: 3}
    zeros = {k: np.zeros_like(v) for k, v in shared.items() if not k.startswith("tb_")}
    zx = np.zeros_like(x[0])
    in_maps = []
    for cid in range(n_cores):
        if cid in place:
            m = dict(shared)
            m["x"] = x[place[cid]]
        else:
            m = dict(shared)
            m.update(zeros)
            m["x"] = zx
        in_maps.append(m)
    res = run_bass_kernel_spmd(nc, in_maps, core_ids=list(range(n_cores)))
    inv = {b: cid for cid, b in place.items()}
    out = np.stack([res.results[inv[b]]["y"] for b in range(B)], axis=0)
    return out.astype(np.float32)
```
